# Optimizing a Trainium2 kernel written in Bass

```python
import math
import jax, jax.numpy as jnp
from jax import lax
import numpy as np

D_MODEL = 1024
BATCH = 4
SEQ = 4096
DEPTH = 1

D_MIX = D_MODEL
D_RNN = D_MIX // 2
RNN_BLOCKS = 8
RNN_BW = D_RNN // RNN_BLOCKS
CONV_W = 4
RG_C = 8.0
D_ATT = D_MIX - D_RNN
HEAD_DIM = 64
N_HEADS = D_ATT // HEAD_DIM
Q_BLOCK = 128
D_FF = 2816
N_IN = 2 * D_RNN + 3 * D_ATT
EPS = 1e-6

kernel_name = "hybrid_rglru_stickbreaking_macaron"


def _rms_norm(x, g):
    xf = x.astype(jnp.float32)
    r = lax.rsqrt(jnp.mean(xf * xf, axis=-1, keepdims=True) + EPS)
    return (xf * r * g.astype(jnp.float32)).astype(x.dtype)


def _swiglu(x, w_gate, w_up, w_down):
    return (jax.nn.silu(x @ w_gate) * (x @ w_up)) @ w_down


def _lin_combine(left, right):
    a1, b1 = left
    a2, b2 = right
    return a1 * a2, a2 * b1 + b2


def _rg_lru_group(xr, gate, conv_w, conv_b, w_a, b_a, w_x, b_x, lam):
    bsz, seq, _ = xr.shape
    kern = conv_w.astype(xr.dtype)[:, None, :]
    xc = lax.conv_general_dilated(
        xr, kern, window_strides=(1,), padding=[(CONV_W - 1, 0)],
        dimension_numbers=("NWC", "WIO", "NWC"), feature_group_count=D_RNN,
    ) + conv_b
    xb = xc.reshape(bsz, seq, RNN_BLOCKS, RNN_BW)
    r = jax.nn.sigmoid(jnp.einsum("bsnc,ncd->bsnd", xb, w_a).reshape(bsz, seq, D_RNN) + b_a)
    i = jax.nn.sigmoid(jnp.einsum("bsnc,ncd->bsnd", xb, w_x).reshape(bsz, seq, D_RNN) + b_x)
    log_a = RG_C * r.astype(jnp.float32) * jax.nn.log_sigmoid(lam.astype(jnp.float32))
    a = jnp.exp(log_a)
    mult = jnp.sqrt(-jnp.expm1(2.0 * log_a))
    b = mult * (i * xc).astype(jnp.float32)
    _, h = lax.associative_scan(_lin_combine, (a, b), axis=1)
    return h.astype(xr.dtype) * jax.nn.gelu(gate)


def _stick_breaking(q, k, v):
    seq = q.shape[2]
    k_pos = jnp.arange(seq)
    kf = k.astype(jnp.float32)
    vf = v.astype(jnp.float32)

    def block(i):
        start = i * Q_BLOCK
        qb = lax.dynamic_slice_in_dim(q, start, Q_BLOCK, axis=2).astype(jnp.float32)
        z = jnp.einsum("bhqd,bhkd->bhqk", qb, kf)
        q_pos = start + jnp.arange(Q_BLOCK)
        causal = k_pos[None, :] < q_pos[:, None]
        log_beta = jax.nn.log_sigmoid(z)
        log_1m = jnp.where(causal, jax.nn.log_sigmoid(-z), 0.0)
        tail = lax.cumsum(log_1m, axis=log_1m.ndim - 1, reverse=True) - log_1m
        w = jnp.where(causal, jnp.exp(log_beta + tail), 0.0)
        return jnp.einsum("bhqk,bhkd->bhqd", w, vf)

    out = lax.map(block, jnp.arange(seq // Q_BLOCK))
    nb, bsz, nh, qb, dh = out.shape
    out = jnp.transpose(out, (1, 0, 3, 2, 4)).reshape(bsz, seq, nh * dh)
    return out.astype(q.dtype)


def setup_inputs(seed: int = 0) -> dict:
    key = jax.random.key(seed)
    ks = jax.random.split(key, 24)
    f32 = jnp.float32

    def nrm(k, shape, scale):
        return jax.random.normal(k, shape, f32) * scale

    def gain(k, n):
        return 1.0 + 0.02 * jax.random.normal(k, (DEPTH, n), f32)

    a_base = jax.random.uniform(ks[13], (DEPTH, D_RNN), f32, 0.9, 0.999)
    s = a_base ** (1.0 / RG_C)
    rg_lambda = jnp.log(s) - jnp.log1p(-s)
    return {
        "x": nrm(ks[0], (BATCH, SEQ, D_MODEL), 1.0),
        "ffn1_norm": gain(ks[1], D_MODEL),
        "ffn1_w_gate": nrm(ks[2], (DEPTH, D_MODEL, D_FF), D_MODEL ** -0.5),
        "ffn1_w_up": nrm(ks[3], (DEPTH, D_MODEL, D_FF), D_MODEL ** -0.5),
        "ffn1_w_down": nrm(ks[4], (DEPTH, D_FF, D_MODEL), D_FF ** -0.5),
        "mix_norm": gain(ks[5], D_MODEL),
        "w_in": nrm(ks[6], (DEPTH, D_MODEL, N_IN), D_MODEL ** -0.5),
        "conv_w": nrm(ks[7], (DEPTH, CONV_W, D_RNN), CONV_W ** -0.5),
        "conv_b": nrm(ks[8], (DEPTH, D_RNN), 0.01),
        "rg_w_a": nrm(ks[9], (DEPTH, RNN_BLOCKS, RNN_BW, RNN_BW), RNN_BW ** -0.5),
        "rg_b_a": nrm(ks[10], (DEPTH, D_RNN), 0.01),
        "rg_w_x": nrm(ks[11], (DEPTH, RNN_BLOCKS, RNN_BW, RNN_BW), RNN_BW ** -0.5),
        "rg_b_x": nrm(ks[12], (DEPTH, D_RNN), 0.01),
        "rg_lambda": rg_lambda,
        "q_norm": gain(ks[14], HEAD_DIM),
        "k_norm": gain(ks[15], HEAD_DIM),
        "rnn_out_norm": gain(ks[16], D_RNN),
        "attn_out_norm": gain(ks[17], D_ATT),
        "w_out": nrm(ks[18], (DEPTH, D_MIX, D_MODEL), D_MIX ** -0.5),
        "ffn2_norm": gain(ks[19], D_MODEL),
        "ffn2_w_gate": nrm(ks[20], (DEPTH, D_MODEL, D_FF), D_MODEL ** -0.5),
        "ffn2_w_up": nrm(ks[21], (DEPTH, D_MODEL, D_FF), D_MODEL ** -0.5),
        "ffn2_w_down": nrm(ks[22], (DEPTH, D_FF, D_MODEL), D_FF ** -0.5),
    }


def reference(x, ffn1_norm, ffn1_w_gate, ffn1_w_up, ffn1_w_down, mix_norm, w_in,
              conv_w, conv_b, rg_w_a, rg_b_a, rg_w_x, rg_b_x, rg_lambda,
              q_norm, k_norm, rnn_out_norm, attn_out_norm, w_out,
              ffn2_norm, ffn2_w_gate, ffn2_w_up, ffn2_w_down):
    bsz, seq, _ = x.shape
    scale = 1.0 / math.sqrt(HEAD_DIM)
    for l in range(DEPTH):
        x = x + 0.5 * _swiglu(_rms_norm(x, ffn1_norm[l]), ffn1_w_gate[l], ffn1_w_up[l], ffn1_w_down[l])

        h = _rms_norm(x, mix_norm[l])
        proj = h @ w_in[l]
        xr, gate, q, k, v = jnp.split(
            proj, [D_RNN, 2 * D_RNN, 2 * D_RNN + D_ATT, 2 * D_RNN + 2 * D_ATT], axis=-1)

        y_rnn = _rg_lru_group(xr, gate, conv_w[l], conv_b[l], rg_w_a[l], rg_b_a[l],
                              rg_w_x[l], rg_b_x[l], rg_lambda[l])

        def heads(t):
            return jnp.transpose(t.reshape(bsz, seq, N_HEADS, HEAD_DIM), (0, 2, 1, 3))
        qh = _rms_norm(heads(q), q_norm[l]) * scale
        kh = _rms_norm(heads(k), k_norm[l])
        y_att = _stick_breaking(qh, kh, heads(v))

        y = jnp.concatenate([_rms_norm(y_rnn, rnn_out_norm[l]),
                             _rms_norm(y_att, attn_out_norm[l])], axis=-1)
        x = x + y @ w_out[l]

        x = x + 0.5 * _swiglu(_rms_norm(x, ffn2_norm[l]), ffn2_w_gate[l], ffn2_w_up[l], ffn2_w_down[l])
    return x
```

```python
from contextlib import ExitStack

import numpy as np
import concourse.bass as bass
import concourse.mybir as mybir
from concourse.bass_utils import run_bass_kernel_spmd

F32 = mybir.dt.float32
BF16 = mybir.dt.bfloat16
AF = mybir.ActivationFunctionType
ALU = mybir.AluOpType

D = 1024
DFF = 2816
NIN = 2560
TT = 512
EPS = 1e-6
ENGS = ["pe", "act", "dve", "pool", "sp"]

PV_FFN1, PV_MIX, PV_FFN2 = 0, 8, 16
PV_CW, PV_CB, PV_BA, PV_BX, PV_LAM, PV_GR, PV_GA, PV_GQ, PV_GK, PV_FLAG = 24, 40, 44, 48, 52, 56, 60, 64, 65, 66
NPV = 67
DV_CL, DV_CL2, DV_NBA, DV_NBX, DV_GQ8, DV_T0, DV_T1, DV_T2 = 0, 4, 8, 12, 16, 17, 21, 25
NDV = 32
CB_ONES1024, CB_ONES512, CB_BD64, CB_TRI, CB_COMPL, CB_MASK = 0, 128, 256, 384, 512, 640
NCB = 640 + 2048


class Op:
    __slots__ = ("eng", "fn", "deps", "flag", "cnt", "dsem", "dval", "fam")


class Sched:
    def __init__(self):
        self.q = {e: [] for e in ENGS}
        self.lastw = {}
        self.readers = {}
        self.dma_cnt = {}

    def add(self, eng, fn, reads=(), writes=(), dsem=None, fam=False, deps=()):
        op = Op()
        op.eng, op.fn, op.dsem, op.flag, op.cnt, op.fam, op.dval = eng, fn, dsem, False, None, fam, None
        ds = set(deps)
        for k in reads:
            w = self.lastw.get(k)
            if w is not None:
                ds.add(w)
        for k in writes:
            w = self.lastw.get(k)
            if w is not None:
                ds.add(w)
            for r in self.readers.get(k, ()):
                ds.add(r)
        if eng == "pe":
            ds = {d for d in ds if not (d.eng == "pe" and d.dsem is None)}
        op.deps = ds
        for k in writes:
            self.lastw[k] = op
            self.readers[k] = []
        for k in reads:
            self.readers.setdefault(k, []).append(op)
        if dsem is not None:
            self.dma_cnt[dsem] = self.dma_cnt.get(dsem, 0) + 1
            op.dval = 16 * self.dma_cnt[dsem]
        self.q[eng].append(op)
        return op

    def finalize(self):
        for e in ENGS:
            for op in self.q[e]:
                for d in op.deps:
                    if d.dsem is None:
                        d.flag = True
        for e in ENGS:
            c = 0
            for op in self.q[e]:
                if op.flag:
                    c += 1
                    op.cnt = c

    def emit(self, ename, eng, sems):
        waited = {}
        for op in self.q[ename]:
            need = {}
            for d in op.deps:
                if d.dsem is not None:
                    s = ("dma", d.dsem)
                    v = 16 * self.dma_cnt[d.dsem] if d.fam else d.dval
                else:
                    s = ("eng", d.eng)
                    v = d.cnt
                if need.get(s, 0) < v:
                    need[s] = v
            for s, v in need.items():
                if waited.get(s, 0) < v:
                    eng.wait_ge(sems[s], v)
                    waited[s] = v
            if op.fn is None:
                continue
            ins = op.fn(eng)
            if op.dsem is not None:
                ins.then_inc(sems[("dma", op.dsem)], 16)
            elif op.flag:
                ins.then_inc(sems[("eng", ename)], 1)


def build(NPRE=4, NOWN=4, debug=False):
    nc = bass.Bass("TRN2", target_bir_lowering=False)
    S = Sched()
    NT = NPRE + NOWN
    NKB = NT * 4
    SP_TOK = NPRE * TT
    SO_TOK = NOWN * TT

    def dram(name, shape, dt, kind="ExternalInput"):
        return nc.dram_tensor(name, shape, dt, kind=kind).ap()

    xpre = dram("xpre", [max(SP_TOK, 128), D], F32)
    xown = dram("xown", [SO_TOK, D], F32)
    pvec = dram("pvec", [128, NPV], F32)
    identd = dram("ident", [128, 128], F32)
    cmat = dram("cmat", [128, NCB], F32)
    bdd = dram("bd", [128, 8, 128], F32)
    wg1 = dram("wg1", [D, DFF], F32)
    wu1 = dram("wu1", [D, DFF], F32)
    wd1 = dram("wd1", [DFF, D], F32)
    win = dram("win", [D, NIN], F32)
    wout = dram("wout", [D, D], F32)
    wg2 = dram("wg2", [D, DFF], F32)
    wu2 = dram("wu2", [D, DFF], F32)
    wd2 = dram("wd2", [DFF, D], F32)
    outd = dram("out", [SO_TOK, D], F32, "ExternalOutput")
    sG = [dram("sG1", [11, 128, 8, 256], BF16, "Internal"), dram("sG2", [11, 128, 8, 256], BF16, "Internal")]
    sU = [dram("sU1", [11, 128, 8, 256], BF16, "Internal"), dram("sU2", [11, 128, 8, 256], BF16, "Internal")]
    sD = [dram("sD1", [8, 128, 11, 256], BF16, "Internal"), dram("sD2", [8, 128, 11, 256], BF16, "Internal")]
    sIn = dram("sIn", [10, 128, 8, 256], BF16, "Internal")
    sOut = dram("sOut", [4, 128, 8, 256], BF16, "Internal")
    dbg = {}
    if debug:
        dbg["x1"] = dram("dbg_x1", [128, 8, TT], F32, "ExternalOutput")
        dbg["kT"] = dram("dbg_kT", [128, 4, NKB * 128], BF16, "ExternalOutput")
        dbg["V"] = dram("dbg_V", [128, NKB, 512], BF16, "ExternalOutput")
        dbg["ynr"] = dram("dbg_ynr", [128, 4, TT], BF16, "ExternalOutput")
        dbg["yna"] = dram("dbg_yna", [128, 4, TT], BF16, "ExternalOutput")
        dbg["x2"] = dram("dbg_x2", [128, 8, TT], F32, "ExternalOutput")
        dbg["qT"] = dram("dbg_qT", [128, 4, TT], BF16, "ExternalOutput")
        dbg["hst"] = dram("dbg_hst", [128, 4], F32, "ExternalOutput")
        dbg["h0"] = dram("dbg_h0", [128, TT], F32, "ExternalOutput")
        dbg["xc0"] = dram("dbg_xc0", [128, TT], F32, "ExternalOutput")
        dbg["r0"] = dram("dbg_r0", [128, TT], F32, "ExternalOutput")
        dbg["i0"] = dram("dbg_i0", [128, TT], F32, "ExternalOutput")

    st = ExitStack()
    with st:
        def sb(name, shape, dt):
            return st.enter_context(nc.sbuf_tensor(name, shape, dt))

        identF = sb("identF", [128, 128], F32)
        cb = sb("cb", [128, NCB], BF16)
        bdf = sb("bdf", [128, 8, 128], F32)
        pv = sb("pv", [128, NPV], F32)
        dv = sb("dv", [128, NDV], F32)
        EPSB = sb("epsb", [128, 1], F32)
        ringA = sb("ringA", [128, 4, 8, 256], BF16)
        ringB = sb("ringB", [128, 2, 11, 256], BF16)
        xin = sb("xin", [128, 2, 1024], F32)
        xT = sb("xT", [128, 8, TT], F32)
        kT = sb("kT", [128, 4, NKB * 128], BF16)
        V = sb("V", [128, NKB, 512], BF16)
        qT = sb("qT", [128, 4, TT], BF16)
        xrb = sb("xrb", [128, 4, TT + 3], F32)
        hst = sb("hst", [128, 4], F32)
        ynr = sb("ynr", [128, 4, TT], BF16)
        FP = sb("FP", [128, 16, TT], F32)
        BP = sb("BP", [128, 32, TT], BF16)
        psT = [st.enter_context(nc.psum_tensor(f"psT{i}", [128, 1024], F32)) for i in range(4)]
        ps = []
        for i in range(4):
            ps.append(psT[i][:, 0:512])
            ps.append(psT[i][:, 512:1024])

        def P(i):
            return FP[:, i, :]

        def Bp(i):
            return BP[:, i, :]

        def pvc(c):
            return pv[:, c:c + 1]

        def dvc(c):
            return dv[:, c:c + 1]

        cmask = cb[:, CB_MASK:CB_MASK + 2048].rearrange("p (i t) -> p i t", i=4)

        S.add("pool", lambda e: e.dma_start(out=identF[:], in_=identd[:, :]), writes=["c_ident"], dsem="setup", fam=True)
        S.add("pool", lambda e: e.dma_start(out=pv[:], in_=pvec[:, :]), writes=["c_pv"], dsem="setup", fam=True)
        for j in range(0, NCB, 512):
            je = min(j + 512, NCB)
            S.add("pool", lambda e, j=j, je=je: e.dma_start(out=cb[:, j:je], in_=cmat[:, j:je]), writes=[("c_cb", j)], dsem="setup", fam=True)
        S.add("pool", lambda e: e.dma_start(out=bdf[:], in_=bdd[:, :, :]), writes=["c_bd"], dsem="setup", fam=True)

        def xload(tt, s):
            src = xpre if tt < NPRE else xown
            r0 = (tt if tt < NPRE else tt - NPRE) * TT + s * 128
            slot = s % 2
            S.add("sp", lambda e: e.dma_start(out=xin[:, slot, :], in_=src[r0:r0 + 128, :]),
                  writes=[("xin", slot)], dsem=("xin", slot))

        xload(0, 0)
        xload(0, 1)

        def conv_A(dst, srcw, g, fam, deps=()):
            src = srcw.rearrange("(kc p) f -> p kc f", p=128)[:, :, g * 256:(g + 1) * 256]
            S.add("pool", lambda e: e.dma_start(out=dst[g], in_=src), writes=[("scr", fam, id(dst), g)], dsem=fam, fam=True, deps=deps)

        def conv_B(dst, srcw, dg, hf, fam, deps=()):
            src = srcw.rearrange("(hf fc p) d -> hf p fc d", hf=2, fc=11, p=128)[hf][:, :, dg * 256:(dg + 1) * 256]
            S.add("pool", lambda e: e.dma_start(out=dst[dg * 2 + hf], in_=src), writes=[("scr", fam, id(dst), dg * 2 + hf)], dsem=fam, fam=True,
                  deps=deps)

        for g in range(11):
            conv_A(sG[0], wg1, g, ("c_a1", g))
            conv_A(sU[0], wu1, g, ("c_a1", g))
        for dg in range(4):
            for hf in range(2):
                conv_B(sD[0], wd1, dg, hf, ("c_b1", dg))
        for u in range(10):
            conv_A(sIn, win, u, "c_in")

        def late_convs():
            d = [S.q["pe"][-1]]
            for u in range(4):
                conv_A(sOut, wout, u, "c_out", d)
            for g in range(11):
                conv_A(sG[1], wg2, g, "c_a2", d)
                conv_A(sU[1], wu2, g, "c_a2", d)
            for dg in range(4):
                for hf in range(2):
                    conv_B(sD[1], wd2, dg, hf, "c_b2", d)

        CST = ["c_ident", "c_pv", "c_bd"] + [("c_cb", j) for j in range(0, NCB, 512)]
        S.add("act", lambda e: e.activation(out=dv[:, DV_T0:DV_T0 + 4], in_=pv[:, PV_LAM:PV_LAM + 4], func=AF.Exp, scale=-1.0),
              reads=["c_pv"], writes=["d_t0"])
        S.add("act", lambda e: e.activation(out=dv[:, DV_T1:DV_T1 + 4], in_=dv[:, DV_T0:DV_T0 + 4], func=AF.Ln, bias=1.0),
              reads=["d_t0"], writes=["d_t1"])
        T0 = dv[:, DV_T0:DV_T0 + 4]
        T1 = dv[:, DV_T1:DV_T1 + 4]
        T2 = dv[:, DV_T2:DV_T2 + 4]
        CL = dv[:, DV_CL:DV_CL + 4]
        CL2 = dv[:, DV_CL2:DV_CL2 + 4]
        S.add("dve", lambda e: e.tensor_scalar(out=T2, in0=T0, scalar1=-0.25, scalar2=1.0 / 3.0, op0=ALU.mult, op1=ALU.add),
              reads=["d_t0"], writes=["d_t2"])
        S.add("dve", lambda e: e.tensor_tensor(out=T2, in0=T2, in1=T0, op=ALU.mult), reads=["d_t2", "d_t0"], writes=["d_t2"])
        S.add("dve", lambda e: e.tensor_scalar(out=T2, in0=T2, scalar1=-1.0, scalar2=0.5, op0=ALU.mult, op1=ALU.add),
              reads=["d_t2"], writes=["d_t2"])
        S.add("dve", lambda e: e.tensor_tensor(out=T2, in0=T2, in1=T0, op=ALU.mult), reads=["d_t2", "d_t0"], writes=["d_t2"])
        S.add("dve", lambda e: e.tensor_scalar(out=T2, in0=T2, scalar1=-1.0, scalar2=1.0, op0=ALU.mult, op1=ALU.add),
              reads=["d_t2"], writes=["d_t2"])
        S.add("dve", lambda e: e.tensor_tensor(out=T2, in0=T2, in1=T0, op=ALU.mult), reads=["d_t2", "d_t0"], writes=["d_t2"])
        S.add("dve", lambda e: e.tensor_tensor(out=T2, in0=T2, in1=T1, op=ALU.subtract), reads=["d_t2", "d_t1"], writes=["d_t2"])
        S.add("dve", lambda e: e.tensor_single_scalar(out=CL2, in_=T0, scalar=0.1, op=ALU.is_lt), reads=["d_t0"], writes=["d_cl2"])
        S.add("dve", lambda e: e.tensor_tensor(out=T2, in0=T2, in1=CL2, op=ALU.mult), reads=["d_t2", "d_cl2"], writes=["d_t2"])
        S.add("dve", lambda e: e.tensor_tensor(out=T2, in0=T2, in1=T1, op=ALU.add), reads=["d_t2", "d_t1"], writes=["d_t2"])
        S.add("dve", lambda e: e.tensor_scalar(out=CL, in0=T2, scalar1=-8.0, scalar2=None, op0=ALU.mult), reads=["d_t2"], writes=["d_cl"])
        S.add("dve", lambda e: e.tensor_scalar(out=CL2, in0=T2, scalar1=-16.0, scalar2=None, op0=ALU.mult), reads=["d_t2"], writes=["d_cl2"])
        S.add("dve", lambda e: e.tensor_scalar(out=dv[:, DV_NBA:DV_NBA + 8], in0=pv[:, PV_BA:PV_BA + 8], scalar1=-1.0, scalar2=None, op0=ALU.mult),
              reads=["c_pv"], writes=["d_nb"])
        S.add("dve", lambda e: e.tensor_scalar(out=dv[:, DV_GQ8:DV_GQ8 + 1], in0=pv[:, PV_GQ:PV_GQ + 1], scalar1=0.125, scalar2=None, op0=ALU.mult),
              reads=["c_pv"], writes=["d_gq"])
        S.add("dve", lambda e: e.memset(EPSB[:], EPS), writes=["d_eps"])
        S.add("dve", lambda e: e.memset(hst[:], 0.0), writes=[("hst", c) for c in range(4)])
        S.add("dve", lambda e: e.memset(xrb[:, :, 0:3], 0.0), writes=[("xrb", c) for c in range(4)])
        DER = ["d_cl", "d_cl2", "d_nb", "d_gq", "d_eps"]
        for en in ("pe", "act", "dve"):
            S.add(en, None, reads=CST + DER)


        def rstd_op(pg, bank):
            S.add("act", lambda e: e.activation(out=P(pg), in_=ps[bank][:], func=AF.Ln, bias=EPSB[:, 0:1]),
                  reads=[("ps", bank)], writes=[("F", pg)])
            S.add("act", lambda e: e.activation(out=P(pg), in_=P(pg), func=AF.Exp, scale=-0.5),
                  reads=[("F", pg)], writes=[("F", pg)])

        def sigm_finish(pg):
            S.add("dve", lambda e: e.tensor_scalar(out=P(pg), in0=P(pg), scalar1=1.0, scalar2=None, op0=ALU.add),
                  reads=[("F", pg)], writes=[("F", pg)])
            S.add("dve", lambda e: e.reciprocal(out=P(pg), in_=P(pg)), reads=[("F", pg)], writes=[("F", pg)])
        ring = {"A": 0, "B": 0}

        def loadA(src_ap, fam, srckey):
            slot = ring["A"] % 4
            ring["A"] += 1
            S.add("sp", lambda e: e.dma_start(out=ringA[:, slot], in_=src_ap), reads=[srckey], writes=[("A", slot)], dsem=("A", slot))
            return slot

        def loadB(src_ap, fam, srckey):
            slot = ring["B"] % 2
            ring["B"] += 1
            S.add("sp", lambda e: e.dma_start(out=ringB[:, slot], in_=src_ap), reads=[srckey], writes=[("RB", slot)], dsem=("Bw", slot))
            return slot

        def mm_group(bank_ap, lhs_list, rhs_list, reads, bankkey, start=True, stop=True, skip=False):
            n = len(lhs_list)

            def fn(e):
                ins = None
                for i in range(n):
                    ins = e.matmul(bank_ap, lhs_list[i], rhs_list[i], start=(start and i == 0), stop=(stop and i == n - 1),
                                   skip_group_check=skip)
                return ins
            S.add("pe", fn, reads=reads, writes=[bankkey])

        cpy_toggle = [0]

        def copy_op(out_ap, in_ap, reads, writes, eng=None):
            if eng is None:
                eng = "act" if cpy_toggle[0] % 2 == 0 else "dve"
                cpy_toggle[0] += 1
            if eng == "act":
                S.add("act", lambda e: e.activation(out=out_ap, in_=in_ap, func=AF.Copy), reads=reads, writes=writes)
            else:
                S.add("dve", lambda e: e.tensor_copy(out=out_ap, in_=in_ap), reads=reads, writes=writes)

        def transposes_in(tt):
            for s in range(4):
                slot = s % 2
                banks = (0, 1) if s % 2 == 0 else (2, 3)

                def fn(e, slot=slot, banks=banks):
                    ins = None
                    for c in range(8):
                        ins = e.transpose(out=ps[banks[c // 4]][:, (c % 4) * 128:(c % 4 + 1) * 128],
                                          in_=xin[:, slot, c * 128:(c + 1) * 128], identity=identF[:])
                    return ins
                S.add("pe", fn, reads=[("xin", slot)], writes=[("ps", banks[0]), ("ps", banks[1])])
                for hb in range(2):
                    copy_op(xT[:, 4 * hb:4 * hb + 4, s * 128:(s + 1) * 128],
                            ps[banks[hb]][:].rearrange("p (c t) -> p c t", c=4),
                            reads=[("ps", banks[hb])], writes=[("xT", c) for c in range(4 * hb, 4 * hb + 4)])
                if s < 2:
                    xload(tt, s + 2)

        def norm_stats():
            for c in range(8):
                sqp = 30 + (c % 2)
                S.add("act", lambda e, c=c, sqp=sqp: e.activation(out=Bp(sqp), in_=xT[:, c, :], func=AF.Square),
                      reads=[("xT", c)], writes=[("B", sqp)])
                mm_group(ps[6][:], [cb[:, CB_ONES1024:CB_ONES1024 + 128]], [Bp(sqp)], [("B", sqp)], ("ps", 6), start=(c == 0), stop=(c == 7))
            rstd_op(0, 6)

        def norm_apply(gcol0):
            for c in range(8):
                S.add("dve", lambda e, c=c: e.scalar_tensor_tensor(out=Bp(c), in0=xT[:, c, :], scalar=pvc(gcol0 + c), in1=P(0),
                                                                   op0=ALU.mult, op1=ALU.mult),
                      reads=[("xT", c), ("F", 0)], writes=[("B", c)])

        def norm_h(gcol0):
            norm_stats()
            norm_apply(gcol0)

        pend = [None]
        defer = [None]

        def pump(n):
            g = pend[0]
            if g is None:
                return
            for _ in range(n):
                try:
                    next(g)
                except StopIteration:
                    pend[0] = None
                    return

        def ffn(which, gcol0, stats_done=False):
            fa = (lambda g: ("c_a1", g)) if which == 0 else (lambda g: "c_a2")
            fb = (lambda dg: ("c_b1", dg)) if which == 0 else (lambda dg: "c_b2")
            if not stats_done:
                norm_stats()
            norm_apply(gcol0)
            hreads = [("B", kc) for kc in range(8)]
            for g in range(11):
                ug = loadA(sG[which][g], fa(g), ("scr", fa(g), id(sG[which]), g))
                uu = loadA(sU[which][g], fa(g), ("scr", fa(g), id(sU[which]), g))
                for fl in range(2):
                    f = 2 * g + fl
                    bg, bu = (0, 1) if f % 2 == 0 else (2, 3)
                    mm_group(ps[bg][:], [ringA[:, ug, kc, fl * 128:(fl + 1) * 128] for kc in range(8)], [Bp(kc) for kc in range(8)],
                             hreads + [("A", ug)], ("ps", bg))
                    mm_group(ps[bu][:], [ringA[:, uu, kc, fl * 128:(fl + 1) * 128] for kc in range(8)], [Bp(kc) for kc in range(8)],
                             hreads + [("A", uu)], ("ps", bu))
                    sp_ = 1 + f % 2
                    S.add("act", lambda e, bg=bg, sp_=sp_: e.activation(out=P(sp_), in_=ps[bg][:], func=AF.Silu),
                          reads=[("ps", bg)], writes=[("F", sp_)])
                    S.add("dve", lambda e, bu=bu, sp_=sp_, f=f: e.tensor_tensor(out=Bp(8 + f), in0=P(sp_), in1=ps[bu][:], op=ALU.mult),
                          reads=[("ps", bu), ("F", sp_)], writes=[("B", 8 + f)])
                    pump(5)
            pump(10000)
            dtt = defer[0]
            defer[0] = None
            if dtt is not None:
                rnn_part1(dtt)
            for dg in range(4):
                banks = (4, 5) if dg % 2 == 0 else (6, 7)
                for hf in range(2):
                    ub = loadB(sD[which][dg * 2 + hf], fb(dg), ("scr", fb(dg), id(sD[which]), dg * 2 + hf))
                    for dl in range(2):
                        mm_group(ps[banks[dl]][:], [ringB[:, ub, fc, dl * 128:(dl + 1) * 128] for fc in range(11)],
                                 [Bp(8 + hf * 11 + fc) for fc in range(11)],
                                 [("B", 8 + hf * 11 + fc) for fc in range(11)] + [("RB", ub)], ("ps", banks[dl]),
                                 start=(hf == 0), stop=(hf == 1))
                for dl in range(2):
                    d = 2 * dg + dl
                    S.add("dve", lambda e, d=d, b=banks[dl]: e.scalar_tensor_tensor(out=xT[:, d, :], in0=ps[b][:], scalar=0.5, in1=xT[:, d, :],
                                                                                   op0=ALU.mult, op1=ALU.add),
                          reads=[("ps", banks[dl]), ("xT", d)], writes=[("xT", d)])
                if dtt is not None and dg == 0:
                    rnn_part2(dtt, [(0, 1), (2, 3), (0, 1), (2, 3)])
                if dtt is not None and dg == 1:
                    rnn_part3(dtt, False)

        rot = [0]

        def next_bank():
            b = rot[0] % 4
            rot[0] += 1
            return b

        hn_toggle = [0]

        def headnorm(bank, scalar_ap, out_ap, outkey):
            t = hn_toggle[0] % 2
            hn_toggle[0] += 1
            sqp = 30 + t
            rsp = 1 + t
            S.add("act", lambda e: e.activation(out=Bp(sqp), in_=ps[bank][:], func=AF.Square), reads=[("ps", bank)], writes=[("B", sqp)])
            mm_group(ps[7][:], [cb[:, CB_BD64:CB_BD64 + 128]], [Bp(sqp)], [("B", sqp)], ("ps", 7))
            rstd_op(rsp, 7)
            S.add("dve", lambda e: e.scalar_tensor_tensor(out=out_ap, in0=ps[bank][:], scalar=scalar_ap, in1=P(rsp), op0=ALU.mult, op1=ALU.mult),
                  reads=[("ps", bank), ("F", rsp)], writes=[outkey])

        def Gp(c):
            return BP[:, 12 + 2 * c:14 + 2 * c, :].rearrange("p a t -> p (a t)").bitcast(F32)

        def Gk(c):
            return [("B", 12 + 2 * c), ("B", 13 + 2 * c)]

        RX = [3, 4, 5, 6]
        RR_ = [7, 8, 9, 10]
        RI = [11, 12, 13, 14]

        def rnn_part1(tt):
            X = RX
            for c in range(4):
                S.add("dve", lambda e, c=c: e.tensor_scalar(out=P(X[c]), in0=xrb[:, c, 0:TT], scalar1=pvc(PV_CW + c), scalar2=pvc(PV_CB + c),
                                                            op0=ALU.mult, op1=ALU.add),
                      reads=[("xrb", c)], writes=[("F", X[c])])
                for j in (1, 2, 3):
                    S.add("dve", lambda e, c=c, j=j: e.scalar_tensor_tensor(out=P(X[c]), in0=xrb[:, c, j:j + TT], scalar=pvc(PV_CW + j * 4 + c),
                                                                            in1=P(X[c]), op0=ALU.mult, op1=ALU.add),
                          reads=[("xrb", c), ("F", X[c])], writes=[("F", X[c])])
                S.add("dve", lambda e, c=c: e.tensor_copy(out=xrb[:, c, 0:3], in_=xrb[:, c, TT:TT + 3]), reads=[("xrb", c)], writes=[("xrb", c)])

        def rnn_part2(tt, gb):
            X, R, I = RX, RR_, RI
            if tt == NPRE - 1:
                dump("xc0", P(X[0]), [("F", X[0])])
            for pair in ((0, 1), (2, 3)):
                for c in pair:
                    mm_group(ps[gb[c][0]][:], [bdf[:, c, :]], [P(X[c])], [("F", X[c])], ("ps", gb[c][0]))
                    mm_group(ps[gb[c][1]][:], [bdf[:, 4 + c, :]], [P(X[c])], [("F", X[c])], ("ps", gb[c][1]))
                for c in pair:
                    S.add("act", lambda e, c=c: e.activation(out=P(R[c]), in_=ps[gb[c][0]][:], func=AF.Sigmoid, bias=pvc(PV_BA + c)),
                          reads=[("ps", gb[c][0])], writes=[("F", R[c])])
                    S.add("act", lambda e, c=c: e.activation(out=P(I[c]), in_=ps[gb[c][1]][:], func=AF.Sigmoid, bias=pvc(PV_BX + c)),
                          reads=[("ps", gb[c][1])], writes=[("F", I[c])])
            if tt == NPRE - 1:
                dump("r0", P(R[0]), [("F", R[0])])
                dump("i0", P(I[0]), [("F", I[0])])
            for c in range(4):
                S.add("dve", lambda e, c=c: e.tensor_tensor(out=P(I[c]), in0=P(I[c]), in1=P(X[c]), op=ALU.mult),
                      reads=[("F", I[c]), ("F", X[c])], writes=[("F", I[c])])
            for c in range(4):
                S.add("act", lambda e, c=c: e.activation(out=P(X[c]), in_=P(R[c]), func=AF.Exp, scale=dvc(DV_CL2 + c)),
                      reads=[("F", R[c])], writes=[("F", X[c])])
            for c in range(4):
                S.add("act", lambda e, c=c: e.activation(out=P(X[c]), in_=P(X[c]), func=AF.Ln, bias=1.0, scale=-1.0),
                      reads=[("F", X[c])], writes=[("F", X[c])])
            for c in range(4):
                S.add("act", lambda e, c=c: e.activation(out=P(X[c]), in_=P(X[c]), func=AF.Exp, scale=0.5),
                      reads=[("F", X[c])], writes=[("F", X[c])])

        def rnn_part3(tt, own):
            X, R, I = RX, RR_, RI
            for c in range(4):
                S.add("dve", lambda e, c=c: e.tensor_tensor(out=P(I[c]), in0=P(I[c]), in1=P(X[c]), op=ALU.mult),
                      reads=[("F", I[c]), ("F", X[c])], writes=[("F", I[c])])
            for c in range(4):
                S.add("act", lambda e, c=c: e.activation(out=P(X[c]), in_=P(R[c]), func=AF.Exp, scale=dvc(DV_CL + c)),
                      reads=[("F", R[c])], writes=[("F", X[c])])
            for c in range(4):
                S.add("dve", lambda e, c=c: e.tensor_tensor_scan(out=P(R[c]), data0=P(X[c]), data1=P(I[c]), initial=hst[:, c:c + 1],
                                                                 op0=ALU.mult, op1=ALU.add),
                      reads=[("F", X[c]), ("F", I[c]), ("hst", c)], writes=[("F", R[c])])
                S.add("dve", lambda e, c=c: e.tensor_copy(out=hst[:, c:c + 1], in_=P(R[c])[:, TT - 1:TT]), reads=[("F", R[c])], writes=[("hst", c)])
            if own:
                for c in range(4):
                    S.add("dve", lambda e, c=c: e.tensor_tensor(out=Gp(c), in0=Gp(c), in1=P(R[c]), op=ALU.mult),
                          reads=Gk(c) + [("F", R[c])], writes=Gk(c))
                    sq = 9 + c % 2
                    S.add("act", lambda e, c=c, sq=sq: e.activation(out=Bp(sq), in_=Gp(c), func=AF.Square), reads=Gk(c), writes=[("B", sq)])
                    mm_group(ps[6][:], [cb[:, CB_ONES512:CB_ONES512 + 128]], [Bp(sq)], [("B", sq)], ("ps", 6), start=(c == 0), stop=(c == 3))
                rstd_op(0, 6)
                for c in range(4):
                    S.add("dve", lambda e, c=c: e.scalar_tensor_tensor(out=ynr[:, c, :], in0=Gp(c), scalar=pvc(PV_GR + c), in1=P(0),
                                                                       op0=ALU.mult, op1=ALU.mult),
                          reads=Gk(c) + [("F", 0)], writes=[("ynr", c)])
            if tt == NPRE - 1:
                S.add("dve", lambda e: e.tensor_scalar(out=hst[:, 0:4], in0=hst[:, 0:4], scalar1=pvc(PV_FLAG), scalar2=None, op0=ALU.mult),
                      reads=[("hst", c) for c in range(4)], writes=[("hst", c) for c in range(4)])
                dump("hst", hst[:], [("hst", c) for c in range(4)])
                dump("h0", P(RR_[0]), [("F", RR_[0])])

        def mixer_proj(tt, own, hook=None):
            norm_h(PV_MIX)
            if hook is not None:
                hook()
            hreads = [("B", kc) for kc in range(8)]
            units = list(range(10)) if own else [0, 1, 6, 7, 8, 9]
            for u in units:
                slot = loadA(sIn[u], "c_in", ("scr", "c_in", id(sIn), u))
                if u < 8:
                    for cl in range(2):
                        ch = 2 * u + cl
                        bank = next_bank()
                        mm_group(ps[bank][:], [ringA[:, slot, kc, cl * 128:(cl + 1) * 128] for kc in range(8)], [Bp(kc) for kc in range(8)],
                                 hreads + [("A", slot)], ("ps", bank))
                        if ch < 4:
                            S.add("act", lambda e, ch=ch, bank=bank: e.activation(out=xrb[:, ch, 3:TT + 3], in_=ps[bank][:], func=AF.Copy),
                                  reads=[("ps", bank)], writes=[("xrb", ch)])
                        elif ch < 8:
                            S.add("act", lambda e, ch=ch, bank=bank: e.activation(out=Gp(ch - 4), in_=ps[bank][:], func=AF.Gelu_apprx_tanh),
                                  reads=[("ps", bank)], writes=Gk(ch - 4))
                        elif ch < 12:
                            hp = ch - 8
                            headnorm(bank, dvc(DV_GQ8), qT[:, hp, :], ("qT", hp))
                        else:
                            hp = ch - 12
                            headnorm(bank, pvc(PV_GK), kT[:, hp, tt * TT:(tt + 1) * TT], ("kT", hp, tt))
                else:
                    for s in range(4):
                        bank = next_bank()
                        kb = tt * 4 + s
                        mm_group(ps[bank][:, 0:256], [Bp(kc)[:, s * 128:(s + 1) * 128] for kc in range(8)],
                                 [ringA[:, slot, kc, :] for kc in range(8)], hreads + [("A", slot)], ("ps", bank))
                        copy_op(V[:, kb, (u - 8) * 256:(u - 7) * 256], ps[bank][:, 0:256], reads=[("ps", bank)], writes=[("V", kb)], eng="act")
                if own and u == 1:
                    rnn_part1(tt)
                if own and u == 5:
                    rnn_part2(tt, [(4, 5), (6, 7), (4, 5), (6, 7)])
                if own and u == 9:
                    rnn_part3(tt, True)
            if not own:
                defer[0] = tt
            return None

        def attention(tt):
            qt = tt - NPRE
            n = NPRE * 4 + 4 * qt + 4
            steps = [(hp, k) for hp in range(4) for k in range(n)]
            NS = len(steps)

            def kb_of(k):
                return n - 1 - k

            def off_of(g):
                k = steps[g][1]
                return 128 * (3 - k) if k < 4 else 0

            def z(g, c):
                hp, k = steps[g]
                kb = kb_of(k)
                o = off_of(g)
                pb = slice(64 * c, 64 * c + 64)
                mm_group(ps[c][:, o:TT], [kT[pb, hp, kb * 128:(kb + 1) * 128]], [qT[pb, hp, o:TT]],
                         [("kT", hp, kb // 4), ("qT", hp)], ("ps", c))

            def e_(g):
                hp, k = steps[g]
                sl = g % 3
                o = off_of(g)
                S.add("act", lambda e: e.activation(out=FP[:, 2 * sl:2 * sl + 2, o:TT],
                                                    in_=psT[0][:].rearrange("p (c t) -> p c t", c=2)[:, :, o:TT], func=AF.Exp),
                      reads=[("ps", 0), ("ps", 1)], writes=[("F", 2 * sl), ("F", 2 * sl + 1)])
                if k < 4:
                    i = 3 - k
                    for c in (0, 1):
                        pg = 2 * sl + c
                        S.add("dve", lambda e, pg=pg: e.tensor_tensor(out=P(pg)[:, o:TT], in0=P(pg)[:, o:TT], in1=cmask[:, i, o:TT], op=ALU.mult),
                              reads=[("F", pg)], writes=[("F", pg)])

            def sp_(g):
                sl = g % 3
                o = off_of(g)
                S.add("act", lambda e: e.activation(out=BP[:, 2 * sl:2 * sl + 2, o:TT], in_=FP[:, 2 * sl:2 * sl + 2, o:TT], func=AF.Ln, bias=1.0),
                      reads=[("F", 2 * sl), ("F", 2 * sl + 1)], writes=[("B", 2 * sl), ("B", 2 * sl + 1)])

            def tri(g, c):
                hp, k = steps[g]
                pg = 2 * (g % 3) + c
                o = off_of(g)
                mm_group(ps[2 + c][:, o:TT], [cb[:, CB_TRI:CB_TRI + 128]], [Bp(pg)[:, o:TT]], [("B", pg)], ("ps", 2 + c),
                         start=(k == 0), stop=True, skip=True)

            def compl(g, c):
                pg = 2 * (g % 3) + c
                o = off_of(g)
                mm_group(ps[2 + c][:, o:TT], [cb[:, CB_COMPL:CB_COMPL + 128]], [Bp(pg)[:, o:TT]], [("B", pg)], ("ps", 2 + c),
                         start=False, stop=True, skip=True)

            def E2(g):
                x = 3 + g % 2
                o = off_of(g)
                S.add("act", lambda e: e.activation(out=FP[:, 2 * x:2 * x + 2, o:TT],
                                                    in_=psT[1][:].rearrange("p (c t) -> p c t", c=2)[:, :, o:TT], func=AF.Exp, scale=-1.0),
                      reads=[("ps", 2), ("ps", 3)], writes=[("F", 2 * x), ("F", 2 * x + 1)])

            def w_(g):
                sl, x = g % 3, 3 + g % 2
                o = off_of(g)
                S.add("dve", lambda e: e.tensor_tensor(out=BP[:, 2 * x:2 * x + 2, o:TT], in0=FP[:, 2 * sl:2 * sl + 2, o:TT],
                                                       in1=FP[:, 2 * x:2 * x + 2, o:TT], op=ALU.mult),
                      reads=[("F", 2 * sl), ("F", 2 * sl + 1), ("F", 2 * x), ("F", 2 * x + 1)], writes=[("B", 2 * x), ("B", 2 * x + 1)])

            def wv(g):
                hp, k = steps[g]
                kb = kb_of(k)
                o = off_of(g)
                obank = 4 + hp % 2
                for c in (0, 1):
                    h = 2 * hp + c
                    pw = 2 * (3 + g % 2) + c
                    mm_group(ps[obank][64 * c:64 * c + 64, o:TT], [V[:, kb, h * 64:(h + 1) * 64]], [Bp(pw)[:, o:TT]],
                             [("V", kb), ("B", pw)], ("ps", obank), start=(k == 0), stop=True, skip=True)
                if k == n - 1:
                    S.add("dve", lambda e: e.tensor_copy(out=P(10 + hp), in_=ps[obank][:]),
                          reads=[("ps", obank)], writes=[("F", 10 + hp)])
                    sq = 10 + hp % 2
                    S.add("dve", lambda e: e.tensor_tensor(out=Bp(sq), in0=P(10 + hp), in1=P(10 + hp), op=ALU.mult),
                          reads=[("F", 10 + hp)], writes=[("B", sq)])
                    mm_group(ps[6][:], [cb[:, CB_ONES512:CB_ONES512 + 128]], [Bp(sq)], [("B", sq)], ("ps", 6), start=(hp == 0), stop=(hp == 3))

            for c in (0, 1):
                z(0, c)
            e_(0)
            sp_(0)
            for g in range(NS):
                hp, k = steps[g]
                if g + 1 < NS:
                    for c in (0, 1):
                        z(g + 1, c)
                if k >= 1:
                    for c in (0, 1):
                        compl(g - 1, c)
                for c in (0, 1):
                    tri(g, c)
                if g >= 1:
                    wv(g - 1)
                if g + 1 < NS:
                    e_(g + 1)
                E2(g)
                if g + 1 < NS:
                    sp_(g + 1)
                w_(g)
            wv(NS - 1)
            rstd_op(14, 6)
            for hp in range(4):
                S.add("dve", lambda e, hp=hp: e.scalar_tensor_tensor(out=Bp(12 + hp), in0=P(10 + hp), scalar=pvc(PV_GA + hp), in1=P(14),
                                                                     op0=ALU.mult, op1=ALU.mult),
                      reads=[("F", 10 + hp), ("F", 14)], writes=[("B", 12 + hp)])

        def w_out_stage():
            yreads = [("ynr", c) for c in range(4)] + [("B", 12 + c) for c in range(4)]
            rhs = [ynr[:, c, :] for c in range(4)] + [Bp(12 + c) for c in range(4)]
            for u in range(4):
                slot = loadA(sOut[u], "c_out", ("scr", "c_out", id(sOut), u))
                for dl in range(2):
                    d = 2 * u + dl
                    bank = next_bank()
                    mm_group(ps[bank][:], [ringA[:, slot, kc, dl * 128:(dl + 1) * 128] for kc in range(8)], rhs,
                             yreads + [("A", slot)], ("ps", bank))
                    S.add("dve", lambda e, d=d, bank=bank: e.tensor_tensor(out=xT[:, d, :], in0=ps[bank][:], in1=xT[:, d, :], op=ALU.add),
                          reads=[("ps", bank), ("xT", d)], writes=[("xT", d)])

        def dump(name, src_ap, reads):
            if debug:
                S.add("pool", lambda e: e.dma_start(out=dbg[name], in_=src_ap), reads=reads, dsem=("dbg", name))

        def output_stage(tt):
            r0 = (tt - NPRE) * TT
            for s in range(4):
                banks = (0, 1) if s % 2 == 0 else (2, 3)
                pa, pb2 = 3 + 2 * s, 4 + 2 * s

                def fn(e, s=s, banks=banks):
                    ins = None
                    for c in range(8):
                        ins = e.transpose(out=ps[banks[c // 4]][:, (c % 4) * 128:(c % 4 + 1) * 128],
                                          in_=xT[:, c, s * 128:(s + 1) * 128], identity=identF[:])
                    return ins
                S.add("pe", fn, reads=[("xT", c) for c in range(8)], writes=[("ps", banks[0]), ("ps", banks[1])])
                copy_op(P(pa), ps[banks[0]][:], reads=[("ps", banks[0])], writes=[("F", pa)], eng="act")
                copy_op(P(pb2), ps[banks[1]][:], reads=[("ps", banks[1])], writes=[("F", pb2)], eng="dve")
                S.add("pool", lambda e, s=s, pa=pa: e.dma_start(out=outd[r0 + s * 128:r0 + (s + 1) * 128, :].rearrange("p (a t) -> p a t", a=2),
                                                                in_=FP[:, pa:pa + 2, :]),
                      reads=[("F", pa), ("F", pb2)], dsem=("os", s))

        front_done = set()

        def tile_front(t):
            transposes_in(t)
            if t + 1 < NT:
                xload(t + 1, 0)
                xload(t + 1, 1)
            norm_stats()
            front_done.add(t)

        for tt in range(NT):
            own = tt >= NPRE
            if tt not in front_done:
                tile_front(tt)
            ffn(0, PV_FFN1, stats_done=True)
            if debug and tt == NPRE:
                dump("x1", xT[:], [("xT", c) for c in range(8)])
            hook = None
            if (not own) and tt + 1 < NT:
                hook = (lambda t=tt + 1: tile_front(t))
            pend[0] = mixer_proj(tt, own, hook)
            if tt == 0:
                late_convs()
            if own:
                pump(10000)
                attention(tt)
                if debug and tt == NPRE:
                    dump("ynr", ynr[:], [("ynr", c) for c in range(4)])
                    dump("yna", BP[:, 12:16, :], [("B", 12 + c) for c in range(4)])
                    dump("qT", qT[:], [("qT", c) for c in range(4)])
                w_out_stage()
                if debug and tt == NPRE:
                    dump("x2", xT[:], [("xT", c) for c in range(8)])
                ffn(1, PV_FFN2)
                output_stage(tt)
        if debug:
            dump("kT", kT[:], [("kT", hp, t) for hp in range(4) for t in range(NT)])
            dump("V", V[:], [("V", kb) for kb in range(NKB)])
        finals = [op for op in S.q["pool"] if op.dsem is not None and (isinstance(op.dsem, tuple) and op.dsem[0] in ("os", "dbg"))]
        S.add("pool", None, deps=finals)

        S.finalize()
        sems = {}
        for e in ENGS:
            sems[("eng", e)] = st.enter_context(nc.semaphore(f"sem_{e}"))
        for i, k in enumerate(sorted(S.dma_cnt.keys(), key=str)):
            sems[("dma", k)] = st.enter_context(nc.semaphore(f"semd_{i}"))
        block = st.enter_context(nc.Block())

        @block.tensor
        def _(e):
            S.emit("pe", e, sems)

        @block.scalar
        def _(e):
            S.emit("act", e, sems)

        @block.vector
        def _(e):
            S.emit("dve", e, sems)

        @block.gpsimd
        def _(e):
            S.emit("pool", e, sems)

        @block.sync
        def _(e):
            S.emit("sp", e, sems)
    return nc


def _consts():
    ident = np.eye(128, dtype=np.float32)
    cm = np.zeros((128, NCB), np.float32)
    cm[:, CB_ONES1024:CB_ONES1024 + 128] = 1.0 / 1024.0
    cm[:, CB_ONES512:CB_ONES512 + 128] = 1.0 / 512.0
    bd = np.zeros((128, 128), np.float32)
    bd[:64, :64] = 1.0 / 64.0
    bd[64:, 64:] = 1.0 / 64.0
    cm[:, CB_BD64:CB_BD64 + 128] = bd
    j = np.arange(128)[:, None]
    s = np.arange(128)[None, :]
    cm[:, CB_TRI:CB_TRI + 128] = (j >= s).astype(np.float32)
    cm[:, CB_COMPL:CB_COMPL + 128] = (j < s).astype(np.float32)
    t = np.arange(512)[None, :]
    for i in range(4):
        cm[:, CB_MASK + i * 512:CB_MASK + (i + 1) * 512] = ((i * 128 + j) < t).astype(np.float32)
    return ident, cm


def _pvec(inp, flag):
    pv = np.zeros((128, NPV), np.float32)

    def cols(v, n):
        return np.asarray(v, np.float32).reshape(n, 128).T

    pv[:, PV_FFN1:PV_FFN1 + 8] = cols(inp["ffn1_norm"][0], 8)
    pv[:, PV_MIX:PV_MIX + 8] = cols(inp["mix_norm"][0], 8)
    pv[:, PV_FFN2:PV_FFN2 + 8] = cols(inp["ffn2_norm"][0], 8)
    cw = np.asarray(inp["conv_w"][0], np.float32)
    for jj in range(4):
        pv[:, PV_CW + jj * 4:PV_CW + jj * 4 + 4] = cols(cw[jj], 4)
    pv[:, PV_CB:PV_CB + 4] = cols(inp["conv_b"][0], 4)
    pv[:, PV_BA:PV_BA + 4] = cols(inp["rg_b_a"][0], 4)
    pv[:, PV_BX:PV_BX + 4] = cols(inp["rg_b_x"][0], 4)
    pv[:, PV_LAM:PV_LAM + 4] = cols(inp["rg_lambda"][0], 4)
    pv[:, PV_GR:PV_GR + 4] = cols(inp["rnn_out_norm"][0], 4)
    pv[:, PV_GA:PV_GA + 4] = cols(inp["attn_out_norm"][0], 4)
    pv[:, PV_GQ] = np.tile(np.asarray(inp["q_norm"][0], np.float32), 2)
    pv[:, PV_GK] = np.tile(np.asarray(inp["k_norm"][0], np.float32), 2)
    pv[:, PV_FLAG] = flag
    return pv


def _bd(inp):
    bd = np.zeros((128, 8, 128), np.float32)
    wa = np.asarray(inp["rg_w_a"][0], np.float32)
    wx = np.asarray(inp["rg_w_x"][0], np.float32)
    for c in range(4):
        for e in range(2):
            bd[64 * e:64 * e + 64, c, 64 * e:64 * e + 64] = wa[2 * c + e]
            bd[64 * e:64 * e + 64, 4 + c, 64 * e:64 * e + 64] = wx[2 * c + e]
    return bd


def make_in_maps(inp, NPRE=4, NOWN=4, ncores=8):
    x = np.asarray(inp["x"], np.float32)
    ident, cm = _consts()
    bd = _bd(inp)
    half = NOWN * TT
    shared = {
        "ident": ident, "cmat": cm, "bd": bd,
        "wg1": np.ascontiguousarray(inp["ffn1_w_gate"][0], np.float32), "wu1": np.ascontiguousarray(inp["ffn1_w_up"][0], np.float32),
        "wd1": np.ascontiguousarray(inp["ffn1_w_down"][0], np.float32), "win": np.ascontiguousarray(inp["w_in"][0], np.float32),
        "wout": np.ascontiguousarray(inp["w_out"][0], np.float32),
        "wg2": np.ascontiguousarray(inp["ffn2_w_gate"][0], np.float32), "wu2": np.ascontiguousarray(inp["ffn2_w_up"][0], np.float32),
        "wd2": np.ascontiguousarray(inp["ffn2_w_down"][0], np.float32),
    }
    pvs = [_pvec(inp, 0.0), _pvec(inp, 1.0)]
    maps = []
    for i in range(ncores):
        b, h = i // 2, i % 2
        m = dict(shared)
        m["xown"] = np.ascontiguousarray(x[b, h * half:(h + 1) * half])
        npre_rows = max(NPRE * TT, 128)
        if h == 1:
            m["xpre"] = np.ascontiguousarray(x[b, 0:npre_rows])
        else:
            m["xpre"] = np.zeros((npre_rows, D), np.float32)
        m["pvec"] = pvs[h]
        maps.append(m)
    return maps


def kernel(**inputs):
    x = np.asarray(inputs["x"])
    B, SEQ, _ = x.shape
    nc = build(4, 4)
    maps = make_in_maps(inputs, 4, 4, 8)
    res = run_bass_kernel_spmd(nc, maps, core_ids=list(range(8)))
    out = np.empty((B, SEQ, D), np.float32)
    half = SEQ // 2
    for i in range(8):
        b, h = i // 2, i % 2
        out[b, h * half:(h + 1) * half] = res.results[i]["out"]
    return out
```

```python
from contextlib import ExitStack

import numpy as np
import concourse.bass as bass
import concourse.mybir as mybir
from concourse.bass_utils import run_bass_kernel_spmd

F32 = mybir.dt.float32
BF16 = mybir.dt.bfloat16
AF = mybir.ActivationFunctionType
ALU = mybir.AluOpType

D = 1024
DFF = 2816
NIN = 2560
TT = 512
EPS = 1e-6
ENGS = ["pe", "act", "dve", "pool", "sp"]

PV_FFN1, PV_MIX, PV_FFN2 = 0, 8, 16
PV_CW, PV_CB, PV_BA, PV_BX, PV_LAM, PV_GR, PV_GA, PV_GQ, PV_GK, PV_FLAG = 24, 40, 44, 48, 52, 56, 60, 64, 65, 66
NPV = 67
DV_CL, DV_CL2, DV_NBA, DV_NBX, DV_GQ8, DV_T0, DV_T1, DV_T2 = 0, 4, 8, 12, 16, 17, 21, 25
NDV = 32
CB_ONES1024, CB_ONES512, CB_BD64, CB_TRI, CB_COMPL, CB_MASK = 0, 128, 256, 384, 512, 640
NCB = 640 + 2048


class Op:
    __slots__ = ("eng", "fn", "deps", "flag", "cnt", "dsem", "dval", "fam")


class Sched:
    def __init__(self):
        self.q = {e: [] for e in ENGS}
        self.lastw = {}
        self.readers = {}
        self.dma_cnt = {}

    def add(self, eng, fn, reads=(), writes=(), dsem=None, fam=False, deps=()):
        op = Op()
        op.eng, op.fn, op.dsem, op.flag, op.cnt, op.fam, op.dval = eng, fn, dsem, False, None, fam, None
        ds = set(deps)
        for k in reads:
            w = self.lastw.get(k)
            if w is not None:
                ds.add(w)
        for k in writes:
            w = self.lastw.get(k)
            if w is not None:
                ds.add(w)
            for r in self.readers.get(k, ()):
                ds.add(r)
        if eng == "pe":
            ds = {d for d in ds if not (d.eng == "pe" and d.dsem is None)}
        op.deps = ds
        for k in writes:
            self.lastw[k] = op
            self.readers[k] = []
        for k in reads:
            self.readers.setdefault(k, []).append(op)
        if dsem is not None:
            self.dma_cnt[dsem] = self.dma_cnt.get(dsem, 0) + 1
            op.dval = 16 * self.dma_cnt[dsem]
        self.q[eng].append(op)
        return op

    def finalize(self):
        for e in ENGS:
            for op in self.q[e]:
                for d in op.deps:
                    if d.dsem is None:
                        d.flag = True
        for e in ENGS:
            c = 0
            for op in self.q[e]:
                if op.flag:
                    c += 1
                    op.cnt = c

    def emit(self, ename, eng, sems):
        waited = {}
        for op in self.q[ename]:
            need = {}
            for d in op.deps:
                if d.dsem is not None:
                    s = ("dma", d.dsem)
                    v = 16 * self.dma_cnt[d.dsem] if d.fam else d.dval
                else:
                    s = ("eng", d.eng)
                    v = d.cnt
                if need.get(s, 0) < v:
                    need[s] = v
            for s, v in need.items():
                if waited.get(s, 0) < v:
                    eng.wait_ge(sems[s], v)
                    waited[s] = v
            if op.fn is None:
                continue
            ins = op.fn(eng)
            if op.dsem is not None:
                ins.then_inc(sems[("dma", op.dsem)], 16)
            elif op.flag:
                ins.then_inc(sems[("eng", ename)], 1)


def build(NPRE=4, NOWN=4, debug=False):
    nc = bass.Bass("TRN2", target_bir_lowering=False)
    S = Sched()
    NT = NPRE + NOWN
    NKB = NT * 4
    SP_TOK = NPRE * TT
    SO_TOK = NOWN * TT

    def dram(name, shape, dt, kind="ExternalInput"):
        return nc.dram_tensor(name, shape, dt, kind=kind).ap()

    xpre = dram("xpre", [max(SP_TOK, 128), D], F32)
    xown = dram("xown", [SO_TOK, D], F32)
    pvec = dram("pvec", [128, NPV], F32)
    identd = dram("ident", [128, 128], F32)
    cmat = dram("cmat", [128, NCB], F32)
    bdd = dram("bd", [128, 8, 128], F32)
    wg1 = dram("wg1", [D, DFF], F32)
    wu1 = dram("wu1", [D, DFF], F32)
    wd1 = dram("wd1", [DFF, D], F32)
    win = dram("win", [D, NIN], F32)
    wout = dram("wout", [D, D], F32)
    wg2 = dram("wg2", [D, DFF], F32)
    wu2 = dram("wu2", [D, DFF], F32)
    wd2 = dram("wd2", [DFF, D], F32)
    outd = dram("out", [SO_TOK, D], F32, "ExternalOutput")
    sG = [dram("sG1", [11, 128, 8, 256], BF16, "Internal"), dram("sG2", [11, 128, 8, 256], BF16, "Internal")]
    sU = [dram("sU1", [11, 128, 8, 256], BF16, "Internal"), dram("sU2", [11, 128, 8, 256], BF16, "Internal")]
    sD = [dram("sD1", [8, 128, 11, 256], BF16, "Internal"), dram("sD2", [8, 128, 11, 256], BF16, "Internal")]
    sIn = dram("sIn", [10, 128, 8, 256], BF16, "Internal")
    sOut = dram("sOut", [4, 128, 8, 256], BF16, "Internal")
    dbg = {}
    if debug:
        dbg["x1"] = dram("dbg_x1", [128, 8, TT], F32, "ExternalOutput")
        dbg["kT"] = dram("dbg_kT", [128, 4, NKB * 128], BF16, "ExternalOutput")
        dbg["V"] = dram("dbg_V", [128, NKB, 512], BF16, "ExternalOutput")
        dbg["ynr"] = dram("dbg_ynr", [128, 4, TT], BF16, "ExternalOutput")
        dbg["yna"] = dram("dbg_yna", [128, 4, TT], BF16, "ExternalOutput")
        dbg["x2"] = dram("dbg_x2", [128, 8, TT], F32, "ExternalOutput")
        dbg["qT"] = dram("dbg_qT", [128, 4, TT], BF16, "ExternalOutput")
        dbg["hst"] = dram("dbg_hst", [128, 4], F32, "ExternalOutput")
        dbg["h0"] = dram("dbg_h0", [128, TT], F32, "ExternalOutput")
        dbg["xc0"] = dram("dbg_xc0", [128, TT], F32, "ExternalOutput")
        dbg["r0"] = dram("dbg_r0", [128, TT], F32, "ExternalOutput")
        dbg["i0"] = dram("dbg_i0", [128, TT], F32, "ExternalOutput")

    st = ExitStack()
    with st:
        def sb(name, shape, dt):
            return st.enter_context(nc.sbuf_tensor(name, shape, dt))

        identF = sb("identF", [128, 128], F32)
        cb = sb("cb", [128, NCB], BF16)
        bdf = sb("bdf", [128, 8, 128], F32)
        pv = sb("pv", [128, NPV], F32)
        dv = sb("dv", [128, NDV], F32)
        EPSB = sb("epsb", [128, 1], F32)
        ringA = sb("ringA", [128, 4, 8, 256], BF16)
        ringB = sb("ringB", [128, 2, 11, 256], BF16)
        xin = sb("xin", [128, 2, 1024], F32)
        xT = sb("xT", [128, 8, TT], F32)
        kT = sb("kT", [128, 4, NKB * 128], BF16)
        V = sb("V", [128, NKB, 512], BF16)
        qT = sb("qT", [128, 4, TT], BF16)
        xrb = sb("xrb", [128, 4, TT + 3], F32)
        hst = sb("hst", [128, 4], F32)
        ynr = sb("ynr", [128, 4, TT], BF16)
        FP = sb("FP", [128, 16, TT], F32)
        BP = sb("BP", [128, 32, TT], BF16)
        psT = [st.enter_context(nc.psum_tensor(f"psT{i}", [128, 1024], F32)) for i in range(4)]
        ps = []
        for i in range(4):
            ps.append(psT[i][:, 0:512])
            ps.append(psT[i][:, 512:1024])

        def P(i):
            return FP[:, i, :]

        def Bp(i):
            return BP[:, i, :]

        def pvc(c):
            return pv[:, c:c + 1]

        def dvc(c):
            return dv[:, c:c + 1]

        cmask = cb[:, CB_MASK:CB_MASK + 2048].rearrange("p (i t) -> p i t", i=4)

        S.add("pool", lambda e: e.dma_start(out=identF[:], in_=identd[:, :]), writes=["c_ident"], dsem="setup", fam=True)
        S.add("pool", lambda e: e.dma_start(out=pv[:], in_=pvec[:, :]), writes=["c_pv"], dsem="setup", fam=True)
        for j in range(0, NCB, 512):
            je = min(j + 512, NCB)
            S.add("pool", lambda e, j=j, je=je: e.dma_start(out=cb[:, j:je], in_=cmat[:, j:je]), writes=[("c_cb", j)], dsem="setup", fam=True)
        S.add("pool", lambda e: e.dma_start(out=bdf[:], in_=bdd[:, :, :]), writes=["c_bd"], dsem="setup", fam=True)

        def xload(tt, s):
            src = xpre if tt < NPRE else xown
            r0 = (tt if tt < NPRE else tt - NPRE) * TT + s * 128
            slot = s % 2
            S.add("sp", lambda e: e.dma_start(out=xin[:, slot, :], in_=src[r0:r0 + 128, :]),
                  writes=[("xin", slot)], dsem=("xin", slot))

        xload(0, 0)
        xload(0, 1)

        def conv_A(dst, srcw, g, fam, deps=()):
            src = srcw.rearrange("(kc p) f -> p kc f", p=128)[:, :, g * 256:(g + 1) * 256]
            S.add("pool", lambda e: e.dma_start(out=dst[g], in_=src), writes=[("scr", fam, id(dst), g)], dsem=fam, fam=True, deps=deps)

        def conv_B(dst, srcw, dg, hf, fam, deps=()):
            src = srcw.rearrange("(hf fc p) d -> hf p fc d", hf=2, fc=11, p=128)[hf][:, :, dg * 256:(dg + 1) * 256]
            S.add("pool", lambda e: e.dma_start(out=dst[dg * 2 + hf], in_=src), writes=[("scr", fam, id(dst), dg * 2 + hf)], dsem=fam, fam=True,
                  deps=deps)

        for g in range(11):
            conv_A(sG[0], wg1, g, ("c_a1", g))
            conv_A(sU[0], wu1, g, ("c_a1", g))
        for dg in range(4):
            for hf in range(2):
                conv_B(sD[0], wd1, dg, hf, ("c_b1", dg))
        for u in range(10):
            conv_A(sIn, win, u, "c_in")

        def late_convs():
            d = [S.q["pe"][-1]]
            for u in range(4):
                conv_A(sOut, wout, u, "c_out", d)
            for g in range(11):
                conv_A(sG[1], wg2, g, "c_a2", d)
                conv_A(sU[1], wu2, g, "c_a2", d)
            for dg in range(4):
                for hf in range(2):
                    conv_B(sD[1], wd2, dg, hf, "c_b2", d)

        CST = ["c_ident", "c_pv", "c_bd"] + [("c_cb", j) for j in range(0, NCB, 512)]
        S.add("act", lambda e: e.activation(out=dv[:, DV_T0:DV_T0 + 4], in_=pv[:, PV_LAM:PV_LAM + 4], func=AF.Exp, scale=-1.0),
              reads=["c_pv"], writes=["d_t0"])
        S.add("act", lambda e: e.activation(out=dv[:, DV_T1:DV_T1 + 4], in_=dv[:, DV_T0:DV_T0 + 4], func=AF.Ln, bias=1.0),
              reads=["d_t0"], writes=["d_t1"])
        T0 = dv[:, DV_T0:DV_T0 + 4]
        T1 = dv[:, DV_T1:DV_T1 + 4]
        T2 = dv[:, DV_T2:DV_T2 + 4]
        CL = dv[:, DV_CL:DV_CL + 4]
        CL2 = dv[:, DV_CL2:DV_CL2 + 4]
        S.add("dve", lambda e: e.tensor_scalar(out=T2, in0=T0, scalar1=-0.25, scalar2=1.0 / 3.0, op0=ALU.mult, op1=ALU.add),
              reads=["d_t0"], writes=["d_t2"])
        S.add("dve", lambda e: e.tensor_tensor(out=T2, in0=T2, in1=T0, op=ALU.mult), reads=["d_t2", "d_t0"], writes=["d_t2"])
        S.add("dve", lambda e: e.tensor_scalar(out=T2, in0=T2, scalar1=-1.0, scalar2=0.5, op0=ALU.mult, op1=ALU.add),
              reads=["d_t2"], writes=["d_t2"])
        S.add("dve", lambda e: e.tensor_tensor(out=T2, in0=T2, in1=T0, op=ALU.mult), reads=["d_t2", "d_t0"], writes=["d_t2"])
        S.add("dve", lambda e: e.tensor_scalar(out=T2, in0=T2, scalar1=-1.0, scalar2=1.0, op0=ALU.mult, op1=ALU.add),
              reads=["d_t2"], writes=["d_t2"])
        S.add("dve", lambda e: e.tensor_tensor(out=T2, in0=T2, in1=T0, op=ALU.mult), reads=["d_t2", "d_t0"], writes=["d_t2"])
        S.add("dve", lambda e: e.tensor_tensor(out=T2, in0=T2, in1=T1, op=ALU.subtract), reads=["d_t2", "d_t1"], writes=["d_t2"])
        S.add("dve", lambda e: e.tensor_single_scalar(out=CL2, in_=T0, scalar=0.1, op=ALU.is_lt), reads=["d_t0"], writes=["d_cl2"])
        S.add("dve", lambda e: e.tensor_tensor(out=T2, in0=T2, in1=CL2, op=ALU.mult), reads=["d_t2", "d_cl2"], writes=["d_t2"])
        S.add("dve", lambda e: e.tensor_tensor(out=T2, in0=T2, in1=T1, op=ALU.add), reads=["d_t2", "d_t1"], writes=["d_t2"])
        S.add("dve", lambda e: e.tensor_scalar(out=CL, in0=T2, scalar1=-8.0, scalar2=None, op0=ALU.mult), reads=["d_t2"], writes=["d_cl"])
        S.add("dve", lambda e: e.tensor_scalar(out=CL2, in0=T2, scalar1=-16.0, scalar2=None, op0=ALU.mult), reads=["d_t2"], writes=["d_cl2"])
        S.add("dve", lambda e: e.tensor_scalar(out=dv[:, DV_NBA:DV_NBA + 8], in0=pv[:, PV_BA:PV_BA + 8], scalar1=-1.0, scalar2=None, op0=ALU.mult),
              reads=["c_pv"], writes=["d_nb"])
        S.add("dve", lambda e: e.tensor_scalar(out=dv[:, DV_GQ8:DV_GQ8 + 1], in0=pv[:, PV_GQ:PV_GQ + 1], scalar1=0.125, scalar2=None, op0=ALU.mult),
              reads=["c_pv"], writes=["d_gq"])
        S.add("dve", lambda e: e.memset(EPSB[:], EPS), writes=["d_eps"])
        S.add("dve", lambda e: e.memset(hst[:], 0.0), writes=[("hst", c) for c in range(4)])
        S.add("dve", lambda e: e.memset(xrb[:, :, 0:3], 0.0), writes=[("xrb", c) for c in range(4)])
        DER = ["d_cl", "d_cl2", "d_nb", "d_gq", "d_eps"]
        for en in ("pe", "act", "dve"):
            S.add(en, None, reads=CST + DER)


        def rstd_op(pg, bank):
            S.add("act", lambda e: e.activation(out=P(pg), in_=ps[bank][:], func=AF.Ln, bias=EPSB[:, 0:1]),
                  reads=[("ps", bank)], writes=[("F", pg)])
            S.add("act", lambda e: e.activation(out=P(pg), in_=P(pg), func=AF.Exp, scale=-0.5),
                  reads=[("F", pg)], writes=[("F", pg)])

        def sigm_finish(pg):
            S.add("dve", lambda e: e.tensor_scalar(out=P(pg), in0=P(pg), scalar1=1.0, scalar2=None, op0=ALU.add),
                  reads=[("F", pg)], writes=[("F", pg)])
            S.add("dve", lambda e: e.reciprocal(out=P(pg), in_=P(pg)), reads=[("F", pg)], writes=[("F", pg)])
        ring = {"A": 0, "B": 0}

        def loadA(src_ap, fam, srckey):
            slot = ring["A"] % 4
            ring["A"] += 1
            S.add("sp", lambda e: e.dma_start(out=ringA[:, slot], in_=src_ap), reads=[srckey], writes=[("A", slot)], dsem=("A", slot))
            return slot

        def loadB(src_ap, fam, srckey):
            slot = ring["B"] % 2
            ring["B"] += 1
            S.add("sp", lambda e: e.dma_start(out=ringB[:, slot], in_=src_ap), reads=[srckey], writes=[("RB", slot)], dsem=("Bw", slot))
            return slot

        def mm_group(bank_ap, lhs_list, rhs_list, reads, bankkey, start=True, stop=True, skip=False):
            n = len(lhs_list)

            def fn(e):
                ins = None
                for i in range(n):
                    ins = e.matmul(bank_ap, lhs_list[i], rhs_list[i], start=(start and i == 0), stop=(stop and i == n - 1),
                                   skip_group_check=skip)
                return ins
            S.add("pe", fn, reads=reads, writes=[bankkey])

        cpy_toggle = [0]

        def copy_op(out_ap, in_ap, reads, writes, eng=None):
            if eng is None:
                eng = "act" if cpy_toggle[0] % 2 == 0 else "dve"
                cpy_toggle[0] += 1
            if eng == "act":
                S.add("act", lambda e: e.activation(out=out_ap, in_=in_ap, func=AF.Copy), reads=reads, writes=writes)
            else:
                S.add("dve", lambda e: e.tensor_copy(out=out_ap, in_=in_ap), reads=reads, writes=writes)

        def transposes_in(tt):
            for s in range(4):
                slot = s % 2
                banks = (0, 1) if s % 2 == 0 else (2, 3)

                def fn(e, slot=slot, banks=banks):
                    ins = None
                    for c in range(8):
                        ins = e.transpose(out=ps[banks[c // 4]][:, (c % 4) * 128:(c % 4 + 1) * 128],
                                          in_=xin[:, slot, c * 128:(c + 1) * 128], identity=identF[:])
                    return ins
                S.add("pe", fn, reads=[("xin", slot)], writes=[("ps", banks[0]), ("ps", banks[1])])
                for hb in range(2):
                    copy_op(xT[:, 4 * hb:4 * hb + 4, s * 128:(s + 1) * 128],
                            ps[banks[hb]][:].rearrange("p (c t) -> p c t", c=4),
                            reads=[("ps", banks[hb])], writes=[("xT", c) for c in range(4 * hb, 4 * hb + 4)])
                if s < 2:
                    xload(tt, s + 2)

        def norm_stats(sq_base):
            for c in range(8):
                sqp = sq_base + c
                S.add("act", lambda e, c=c, sqp=sqp: e.activation(out=Bp(sqp), in_=xT[:, c, :], func=AF.Square),
                      reads=[("xT", c)], writes=[("B", sqp)])
                mm_group(ps[6][:], [cb[:, CB_ONES1024:CB_ONES1024 + 128]], [Bp(sqp)], [("B", sqp)], ("ps", 6), start=(c == 0), stop=(c == 7))
            rstd_op(0, 6)

        def norm_apply(gcol0):
            for c in range(8):
                S.add("dve", lambda e, c=c: e.scalar_tensor_tensor(out=Bp(c), in0=xT[:, c, :], scalar=pvc(gcol0 + c), in1=P(0),
                                                                   op0=ALU.mult, op1=ALU.mult),
                      reads=[("xT", c), ("F", 0)], writes=[("B", c)])

        def norm_h(gcol0):
            norm_stats(0)
            norm_apply(gcol0)

        pend = [None]
        defer = [None]

        def pump(n):
            g = pend[0]
            if g is None:
                return
            for _ in range(n):
                try:
                    next(g)
                except StopIteration:
                    pend[0] = None
                    return

        def ffn(which, gcol0, stats_done=False):
            fa = (lambda g: ("c_a1", g)) if which == 0 else (lambda g: "c_a2")
            fb = (lambda dg: ("c_b1", dg)) if which == 0 else (lambda dg: "c_b2")
            if not stats_done:
                norm_stats(16)
            norm_apply(gcol0)
            hreads = [("B", kc) for kc in range(8)]
            for g in range(11):
                ug = loadA(sG[which][g], fa(g), ("scr", fa(g), id(sG[which]), g))
                uu = loadA(sU[which][g], fa(g), ("scr", fa(g), id(sU[which]), g))
                for fl in range(2):
                    f = 2 * g + fl
                    bg, bu = (0, 1) if f % 2 == 0 else (2, 3)
                    mm_group(ps[bg][:], [ringA[:, ug, kc, fl * 128:(fl + 1) * 128] for kc in range(8)], [Bp(kc) for kc in range(8)],
                             hreads + [("A", ug)], ("ps", bg))
                    mm_group(ps[bu][:], [ringA[:, uu, kc, fl * 128:(fl + 1) * 128] for kc in range(8)], [Bp(kc) for kc in range(8)],
                             hreads + [("A", uu)], ("ps", bu))
                    sp_ = 1 + f % 2
                    S.add("act", lambda e, bg=bg, sp_=sp_: e.activation(out=P(sp_), in_=ps[bg][:], func=AF.Silu),
                          reads=[("ps", bg)], writes=[("F", sp_)])
                    S.add("dve", lambda e, bu=bu, sp_=sp_, f=f: e.tensor_tensor(out=Bp(8 + f), in0=P(sp_), in1=ps[bu][:], op=ALU.mult),
                          reads=[("ps", bu), ("F", sp_)], writes=[("B", 8 + f)])
                    pump(5)
            pump(10000)
            dtt = defer[0]
            defer[0] = None
            if dtt is not None:
                rnn_part1(dtt)
            for dg in range(4):
                banks = (4, 5) if dg % 2 == 0 else (6, 7)
                for hf in range(2):
                    ub = loadB(sD[which][dg * 2 + hf], fb(dg), ("scr", fb(dg), id(sD[which]), dg * 2 + hf))
                    for dl in range(2):
                        mm_group(ps[banks[dl]][:], [ringB[:, ub, fc, dl * 128:(dl + 1) * 128] for fc in range(11)],
                                 [Bp(8 + hf * 11 + fc) for fc in range(11)],
                                 [("B", 8 + hf * 11 + fc) for fc in range(11)] + [("RB", ub)], ("ps", banks[dl]),
                                 start=(hf == 0), stop=(hf == 1))
                for dl in range(2):
                    d = 2 * dg + dl
                    S.add("dve", lambda e, d=d, b=banks[dl]: e.scalar_tensor_tensor(out=xT[:, d, :], in0=ps[b][:], scalar=0.5, in1=xT[:, d, :],
                                                                                   op0=ALU.mult, op1=ALU.add),
                          reads=[("ps", banks[dl]), ("xT", d)], writes=[("xT", d)])
                if dtt is not None and dg == 0:
                    rnn_part2(dtt, [(0, 1), (2, 3), (0, 1), (2, 3)])
                if dtt is not None and dg == 1:
                    rnn_part3(dtt, False)

        rot = [0]

        def next_bank():
            b = rot[0] % 4
            rot[0] += 1
            return b

        hn_toggle = [0]

        def headnorm(bank, scalar_ap, out_ap, outkey):
            t = hn_toggle[0] % 2
            hn_toggle[0] += 1
            sqp = 30 + t
            rsp = 1 + t
            S.add("act", lambda e: e.activation(out=Bp(sqp), in_=ps[bank][:], func=AF.Square), reads=[("ps", bank)], writes=[("B", sqp)])
            mm_group(ps[7][:], [cb[:, CB_BD64:CB_BD64 + 128]], [Bp(sqp)], [("B", sqp)], ("ps", 7))
            rstd_op(rsp, 7)
            S.add("dve", lambda e: e.scalar_tensor_tensor(out=out_ap, in0=ps[bank][:], scalar=scalar_ap, in1=P(rsp), op0=ALU.mult, op1=ALU.mult),
                  reads=[("ps", bank), ("F", rsp)], writes=[outkey])

        def Gp(c):
            return BP[:, 12 + 2 * c:14 + 2 * c, :].rearrange("p a t -> p (a t)").bitcast(F32)

        def Gk(c):
            return [("B", 12 + 2 * c), ("B", 13 + 2 * c)]

        RX = [3, 4, 5, 6]
        RR_ = [7, 8, 9, 10]
        RI = [11, 12, 13, 14]

        def rnn_part1(tt):
            X = RX
            for c in range(4):
                S.add("dve", lambda e, c=c: e.tensor_scalar(out=P(X[c]), in0=xrb[:, c, 0:TT], scalar1=pvc(PV_CW + c), scalar2=pvc(PV_CB + c),
                                                            op0=ALU.mult, op1=ALU.add),
                      reads=[("xrb", c)], writes=[("F", X[c])])
                for j in (1, 2, 3):
                    S.add("dve", lambda e, c=c, j=j: e.scalar_tensor_tensor(out=P(X[c]), in0=xrb[:, c, j:j + TT], scalar=pvc(PV_CW + j * 4 + c),
                                                                            in1=P(X[c]), op0=ALU.mult, op1=ALU.add),
                          reads=[("xrb", c), ("F", X[c])], writes=[("F", X[c])])
                S.add("dve", lambda e, c=c: e.tensor_copy(out=xrb[:, c, 0:3], in_=xrb[:, c, TT:TT + 3]), reads=[("xrb", c)], writes=[("xrb", c)])

        def rnn_part2(tt, gb):
            X, R, I = RX, RR_, RI
            if tt == NPRE - 1:
                dump("xc0", P(X[0]), [("F", X[0])])
            for pair in ((0, 1), (2, 3)):
                for c in pair:
                    mm_group(ps[gb[c][0]][:], [bdf[:, c, :]], [P(X[c])], [("F", X[c])], ("ps", gb[c][0]))
                    mm_group(ps[gb[c][1]][:], [bdf[:, 4 + c, :]], [P(X[c])], [("F", X[c])], ("ps", gb[c][1]))
                for c in pair:
                    S.add("act", lambda e, c=c: e.activation(out=P(R[c]), in_=ps[gb[c][0]][:], func=AF.Sigmoid, bias=pvc(PV_BA + c)),
                          reads=[("ps", gb[c][0])], writes=[("F", R[c])])
                    S.add("act", lambda e, c=c: e.activation(out=P(I[c]), in_=ps[gb[c][1]][:], func=AF.Sigmoid, bias=pvc(PV_BX + c)),
                          reads=[("ps", gb[c][1])], writes=[("F", I[c])])
            if tt == NPRE - 1:
                dump("r0", P(R[0]), [("F", R[0])])
                dump("i0", P(I[0]), [("F", I[0])])
            for c in range(4):
                S.add("dve", lambda e, c=c: e.tensor_tensor(out=P(I[c]), in0=P(I[c]), in1=P(X[c]), op=ALU.mult),
                      reads=[("F", I[c]), ("F", X[c])], writes=[("F", I[c])])
            for c in range(4):
                S.add("act", lambda e, c=c: e.activation(out=P(X[c]), in_=P(R[c]), func=AF.Exp, scale=dvc(DV_CL2 + c)),
                      reads=[("F", R[c])], writes=[("F", X[c])])
            for c in range(4):
                S.add("act", lambda e, c=c: e.activation(out=P(X[c]), in_=P(X[c]), func=AF.Ln, bias=1.0, scale=-1.0),
                      reads=[("F", X[c])], writes=[("F", X[c])])
            for c in range(4):
                S.add("act", lambda e, c=c: e.activation(out=P(X[c]), in_=P(X[c]), func=AF.Exp, scale=0.5),
                      reads=[("F", X[c])], writes=[("F", X[c])])

        def rnn_part3(tt, own):
            X, R, I = RX, RR_, RI
            for c in range(4):
                S.add("dve", lambda e, c=c: e.tensor_tensor(out=P(I[c]), in0=P(I[c]), in1=P(X[c]), op=ALU.mult),
                      reads=[("F", I[c]), ("F", X[c])], writes=[("F", I[c])])
            for c in range(4):
                S.add("act", lambda e, c=c: e.activation(out=P(X[c]), in_=P(R[c]), func=AF.Exp, scale=dvc(DV_CL + c)),
                      reads=[("F", R[c])], writes=[("F", X[c])])
            for c in range(4):
                S.add("dve", lambda e, c=c: e.tensor_tensor_scan(out=P(R[c]), data0=P(X[c]), data1=P(I[c]), initial=hst[:, c:c + 1],
                                                                 op0=ALU.mult, op1=ALU.add),
                      reads=[("F", X[c]), ("F", I[c]), ("hst", c)], writes=[("F", R[c])])
                S.add("dve", lambda e, c=c: e.tensor_copy(out=hst[:, c:c + 1], in_=P(R[c])[:, TT - 1:TT]), reads=[("F", R[c])], writes=[("hst", c)])
            if own:
                for c in range(4):
                    S.add("dve", lambda e, c=c: e.tensor_tensor(out=Gp(c), in0=Gp(c), in1=P(R[c]), op=ALU.mult),
                          reads=Gk(c) + [("F", R[c])], writes=Gk(c))
                    sq = 9 + c % 2
                    S.add("act", lambda e, c=c, sq=sq: e.activation(out=Bp(sq), in_=Gp(c), func=AF.Square), reads=Gk(c), writes=[("B", sq)])
                    mm_group(ps[6][:], [cb[:, CB_ONES512:CB_ONES512 + 128]], [Bp(sq)], [("B", sq)], ("ps", 6), start=(c == 0), stop=(c == 3))
                rstd_op(0, 6)
                for c in range(4):
                    S.add("dve", lambda e, c=c: e.scalar_tensor_tensor(out=ynr[:, c, :], in0=Gp(c), scalar=pvc(PV_GR + c), in1=P(0),
                                                                       op0=ALU.mult, op1=ALU.mult),
                          reads=Gk(c) + [("F", 0)], writes=[("ynr", c)])
            if tt == NPRE - 1:
                S.add("dve", lambda e: e.tensor_scalar(out=hst[:, 0:4], in0=hst[:, 0:4], scalar1=pvc(PV_FLAG), scalar2=None, op0=ALU.mult),
                      reads=[("hst", c) for c in range(4)], writes=[("hst", c) for c in range(4)])
                dump("hst", hst[:], [("hst", c) for c in range(4)])
                dump("h0", P(RR_[0]), [("F", RR_[0])])

        def mixer_proj(tt, own, hook=None):
            norm_h(PV_MIX)
            if hook is not None:
                hook()
            hreads = [("B", kc) for kc in range(8)]
            units = list(range(10)) if own else [0, 1, 6, 7, 8, 9]
            for u in units:
                slot = loadA(sIn[u], "c_in", ("scr", "c_in", id(sIn), u))
                if u < 8:
                    for cl in range(2):
                        ch = 2 * u + cl
                        bank = next_bank()
                        mm_group(ps[bank][:], [ringA[:, slot, kc, cl * 128:(cl + 1) * 128] for kc in range(8)], [Bp(kc) for kc in range(8)],
                                 hreads + [("A", slot)], ("ps", bank))
                        if ch < 4:
                            S.add("act", lambda e, ch=ch, bank=bank: e.activation(out=xrb[:, ch, 3:TT + 3], in_=ps[bank][:], func=AF.Copy),
                                  reads=[("ps", bank)], writes=[("xrb", ch)])
                        elif ch < 8:
                            S.add("act", lambda e, ch=ch, bank=bank: e.activation(out=Gp(ch - 4), in_=ps[bank][:], func=AF.Gelu_apprx_tanh),
                                  reads=[("ps", bank)], writes=Gk(ch - 4))
                        elif ch < 12:
                            hp = ch - 8
                            headnorm(bank, dvc(DV_GQ8), qT[:, hp, :], ("qT", hp))
                        else:
                            hp = ch - 12
                            headnorm(bank, pvc(PV_GK), kT[:, hp, tt * TT:(tt + 1) * TT], ("kT", hp, tt))
                else:
                    for s in range(4):
                        bank = next_bank()
                        kb = tt * 4 + s
                        mm_group(ps[bank][:, 0:256], [Bp(kc)[:, s * 128:(s + 1) * 128] for kc in range(8)],
                                 [ringA[:, slot, kc, :] for kc in range(8)], hreads + [("A", slot)], ("ps", bank))
                        copy_op(V[:, kb, (u - 8) * 256:(u - 7) * 256], ps[bank][:, 0:256], reads=[("ps", bank)], writes=[("V", kb)], eng="act")
                if own and u == 1:
                    rnn_part1(tt)
                if own and u == 5:
                    rnn_part2(tt, [(4, 5), (6, 7), (4, 5), (6, 7)])
                if own and u == 9:
                    rnn_part3(tt, True)
            if not own:
                defer[0] = tt
            return None

        def attention(tt):
            qt = tt - NPRE
            n = NPRE * 4 + 4 * qt + 4
            steps = [(hp, k) for hp in range(4) for k in range(n)]
            NS = len(steps)

            def kb_of(k):
                return n - 1 - k

            def off_of(g):
                k = steps[g][1]
                return 128 * (3 - k) if k < 4 else 0

            def z(g, c):
                hp, k = steps[g]
                kb = kb_of(k)
                o = off_of(g)
                pb = slice(64 * c, 64 * c + 64)
                mm_group(ps[c][:, o:TT], [kT[pb, hp, kb * 128:(kb + 1) * 128]], [qT[pb, hp, o:TT]],
                         [("kT", hp, kb // 4), ("qT", hp)], ("ps", c))

            def e_(g):
                hp, k = steps[g]
                sl = g % 3
                o = off_of(g)
                S.add("act", lambda e: e.activation(out=FP[:, 2 * sl:2 * sl + 2, o:TT],
                                                    in_=psT[0][:].rearrange("p (c t) -> p c t", c=2)[:, :, o:TT], func=AF.Exp),
                      reads=[("ps", 0), ("ps", 1)], writes=[("F", 2 * sl), ("F", 2 * sl + 1)])
                if k < 4:
                    i = 3 - k
                    for c in (0, 1):
                        pg = 2 * sl + c
                        S.add("dve", lambda e, pg=pg: e.tensor_tensor(out=P(pg)[:, o:TT], in0=P(pg)[:, o:TT], in1=cmask[:, i, o:TT], op=ALU.mult),
                              reads=[("F", pg)], writes=[("F", pg)])

            def sp_(g):
                sl = g % 3
                o = off_of(g)
                S.add("act", lambda e: e.activation(out=BP[:, 2 * sl:2 * sl + 2, o:TT], in_=FP[:, 2 * sl:2 * sl + 2, o:TT], func=AF.Ln, bias=1.0),
                      reads=[("F", 2 * sl), ("F", 2 * sl + 1)], writes=[("B", 2 * sl), ("B", 2 * sl + 1)])

            def tri(g, c):
                hp, k = steps[g]
                pg = 2 * (g % 3) + c
                o = off_of(g)
                mm_group(ps[2 + c][:, o:TT], [cb[:, CB_TRI:CB_TRI + 128]], [Bp(pg)[:, o:TT]], [("B", pg)], ("ps", 2 + c),
                         start=(k == 0), stop=True, skip=True)

            def compl(g, c):
                pg = 2 * (g % 3) + c
                o = off_of(g)
                mm_group(ps[2 + c][:, o:TT], [cb[:, CB_COMPL:CB_COMPL + 128]], [Bp(pg)[:, o:TT]], [("B", pg)], ("ps", 2 + c),
                         start=False, stop=True, skip=True)

            def E2(g):
                x = 3 + g % 2
                o = off_of(g)
                S.add("act", lambda e: e.activation(out=FP[:, 2 * x:2 * x + 2, o:TT],
                                                    in_=psT[1][:].rearrange("p (c t) -> p c t", c=2)[:, :, o:TT], func=AF.Exp, scale=-1.0),
                      reads=[("ps", 2), ("ps", 3)], writes=[("F", 2 * x), ("F", 2 * x + 1)])

            def w_(g):
                sl, x = g % 3, 3 + g % 2
                o = off_of(g)
                S.add("dve", lambda e: e.tensor_tensor(out=BP[:, 2 * x:2 * x + 2, o:TT], in0=FP[:, 2 * sl:2 * sl + 2, o:TT],
                                                       in1=FP[:, 2 * x:2 * x + 2, o:TT], op=ALU.mult),
                      reads=[("F", 2 * sl), ("F", 2 * sl + 1), ("F", 2 * x), ("F", 2 * x + 1)], writes=[("B", 2 * x), ("B", 2 * x + 1)])

            def wv(g):
                hp, k = steps[g]
                kb = kb_of(k)
                o = off_of(g)
                obank = 4 + hp % 2
                for c in (0, 1):
                    h = 2 * hp + c
                    pw = 2 * (3 + g % 2) + c
                    mm_group(ps[obank][64 * c:64 * c + 64, o:TT], [V[:, kb, h * 64:(h + 1) * 64]], [Bp(pw)[:, o:TT]],
                             [("V", kb), ("B", pw)], ("ps", obank), start=(k == 0), stop=True, skip=True)
                if k == n - 1:
                    S.add("act", lambda e: e.activation(out=P(10 + hp), in_=ps[obank][:], func=AF.Copy),
                          reads=[("ps", obank)], writes=[("F", 10 + hp)])
                    sq = 10 + hp % 2
                    S.add("act", lambda e: e.activation(out=Bp(sq), in_=P(10 + hp), func=AF.Square),
                          reads=[("F", 10 + hp)], writes=[("B", sq)])
                    mm_group(ps[6][:], [cb[:, CB_ONES512:CB_ONES512 + 128]], [Bp(sq)], [("B", sq)], ("ps", 6), start=(hp == 0), stop=(hp == 3))

            for c in (0, 1):
                z(0, c)
            e_(0)
            sp_(0)
            for g in range(NS):
                hp, k = steps[g]
                if g + 1 < NS:
                    for c in (0, 1):
                        z(g + 1, c)
                if k >= 1:
                    for c in (0, 1):
                        compl(g - 1, c)
                for c in (0, 1):
                    tri(g, c)
                if g >= 1:
                    wv(g - 1)
                if g + 1 < NS:
                    e_(g + 1)
                E2(g)
                if g + 1 < NS:
                    sp_(g + 1)
                w_(g)
            wv(NS - 1)
            rstd_op(14, 6)
            for hp in range(4):
                S.add("dve", lambda e, hp=hp: e.scalar_tensor_tensor(out=Bp(12 + hp), in0=P(10 + hp), scalar=pvc(PV_GA + hp), in1=P(14),
                                                                     op0=ALU.mult, op1=ALU.mult),
                      reads=[("F", 10 + hp), ("F", 14)], writes=[("B", 12 + hp)])

        def w_out_stage():
            yreads = [("ynr", c) for c in range(4)] + [("B", 12 + c) for c in range(4)]
            rhs = [ynr[:, c, :] for c in range(4)] + [Bp(12 + c) for c in range(4)]
            for u in range(4):
                slot = loadA(sOut[u], "c_out", ("scr", "c_out", id(sOut), u))
                for dl in range(2):
                    d = 2 * u + dl
                    bank = next_bank()
                    mm_group(ps[bank][:], [ringA[:, slot, kc, dl * 128:(dl + 1) * 128] for kc in range(8)], rhs,
                             yreads + [("A", slot)], ("ps", bank))
                    S.add("dve", lambda e, d=d, bank=bank: e.tensor_tensor(out=xT[:, d, :], in0=ps[bank][:], in1=xT[:, d, :], op=ALU.add),
                          reads=[("ps", bank), ("xT", d)], writes=[("xT", d)])

        def dump(name, src_ap, reads):
            if debug:
                S.add("pool", lambda e: e.dma_start(out=dbg[name], in_=src_ap), reads=reads, dsem=("dbg", name))

        def output_stage(tt):
            r0 = (tt - NPRE) * TT
            for s in range(4):
                banks = (0, 1) if s % 2 == 0 else (2, 3)
                pa, pb2 = 3 + 2 * s, 4 + 2 * s

                def fn(e, s=s, banks=banks):
                    ins = None
                    for c in range(8):
                        ins = e.transpose(out=ps[banks[c // 4]][:, (c % 4) * 128:(c % 4 + 1) * 128],
                                          in_=xT[:, c, s * 128:(s + 1) * 128], identity=identF[:])
                    return ins
                S.add("pe", fn, reads=[("xT", c) for c in range(8)], writes=[("ps", banks[0]), ("ps", banks[1])])
                copy_op(P(pa), ps[banks[0]][:], reads=[("ps", banks[0])], writes=[("F", pa)], eng="act")
                copy_op(P(pb2), ps[banks[1]][:], reads=[("ps", banks[1])], writes=[("F", pb2)], eng="dve")
                S.add("pool", lambda e, s=s, pa=pa: e.dma_start(out=outd[r0 + s * 128:r0 + (s + 1) * 128, :].rearrange("p (a t) -> p a t", a=2),
                                                                in_=FP[:, pa:pa + 2, :]),
                      reads=[("F", pa), ("F", pb2)], dsem=("os", s))

        front_done = set()

        def tile_front(t):
            transposes_in(t)
            if t + 1 < NT:
                xload(t + 1, 0)
                xload(t + 1, 1)
            norm_stats(16)
            front_done.add(t)

        for tt in range(NT):
            own = tt >= NPRE
            if tt not in front_done:
                tile_front(tt)
            ffn(0, PV_FFN1, stats_done=True)
            if debug and tt == NPRE:
                dump("x1", xT[:], [("xT", c) for c in range(8)])
            hook = None
            if (not own) and tt + 1 < NT:
                hook = (lambda t=tt + 1: tile_front(t))
            pend[0] = mixer_proj(tt, own, hook)
            if tt == 0:
                late_convs()
            if own:
                pump(10000)
                attention(tt)
                if debug and tt == NPRE:
                    dump("ynr", ynr[:], [("ynr", c) for c in range(4)])
                    dump("yna", BP[:, 12:16, :], [("B", 12 + c) for c in range(4)])
                    dump("qT", qT[:], [("qT", c) for c in range(4)])
                w_out_stage()
                if debug and tt == NPRE:
                    dump("x2", xT[:], [("xT", c) for c in range(8)])
                ffn(1, PV_FFN2)
                output_stage(tt)
        if debug:
            dump("kT", kT[:], [("kT", hp, t) for hp in range(4) for t in range(NT)])
            dump("V", V[:], [("V", kb) for kb in range(NKB)])
        finals = [op for op in S.q["pool"] if op.dsem is not None and (isinstance(op.dsem, tuple) and op.dsem[0] in ("os", "dbg"))]
        S.add("pool", None, deps=finals)

        S.finalize()
        sems = {}
        for e in ENGS:
            sems[("eng", e)] = st.enter_context(nc.semaphore(f"sem_{e}"))
        for i, k in enumerate(sorted(S.dma_cnt.keys(), key=str)):
            sems[("dma", k)] = st.enter_context(nc.semaphore(f"semd_{i}"))
        block = st.enter_context(nc.Block())

        @block.tensor
        def _(e):
            S.emit("pe", e, sems)

        @block.scalar
        def _(e):
            S.emit("act", e, sems)

        @block.vector
        def _(e):
            S.emit("dve", e, sems)

        @block.gpsimd
        def _(e):
            S.emit("pool", e, sems)

        @block.sync
        def _(e):
            S.emit("sp", e, sems)
    return nc


def _consts():
    ident = np.eye(128, dtype=np.float32)
    cm = np.zeros((128, NCB), np.float32)
    cm[:, CB_ONES1024:CB_ONES1024 + 128] = 1.0 / 1024.0
    cm[:, CB_ONES512:CB_ONES512 + 128] = 1.0 / 512.0
    bd = np.zeros((128, 128), np.float32)
    bd[:64, :64] = 1.0 / 64.0
    bd[64:, 64:] = 1.0 / 64.0
    cm[:, CB_BD64:CB_BD64 + 128] = bd
    j = np.arange(128)[:, None]
    s = np.arange(128)[None, :]
    cm[:, CB_TRI:CB_TRI + 128] = (j >= s).astype(np.float32)
    cm[:, CB_COMPL:CB_COMPL + 128] = (j < s).astype(np.float32)
    t = np.arange(512)[None, :]
    for i in range(4):
        cm[:, CB_MASK + i * 512:CB_MASK + (i + 1) * 512] = ((i * 128 + j) < t).astype(np.float32)
    return ident, cm


def _pvec(inp, flag):
    pv = np.zeros((128, NPV), np.float32)

    def cols(v, n):
        return np.asarray(v, np.float32).reshape(n, 128).T

    pv[:, PV_FFN1:PV_FFN1 + 8] = cols(inp["ffn1_norm"][0], 8)
    pv[:, PV_MIX:PV_MIX + 8] = cols(inp["mix_norm"][0], 8)
    pv[:, PV_FFN2:PV_FFN2 + 8] = cols(inp["ffn2_norm"][0], 8)
    cw = np.asarray(inp["conv_w"][0], np.float32)
    for jj in range(4):
        pv[:, PV_CW + jj * 4:PV_CW + jj * 4 + 4] = cols(cw[jj], 4)
    pv[:, PV_CB:PV_CB + 4] = cols(inp["conv_b"][0], 4)
    pv[:, PV_BA:PV_BA + 4] = cols(inp["rg_b_a"][0], 4)
    pv[:, PV_BX:PV_BX + 4] = cols(inp["rg_b_x"][0], 4)
    pv[:, PV_LAM:PV_LAM + 4] = cols(inp["rg_lambda"][0], 4)
    pv[:, PV_GR:PV_GR + 4] = cols(inp["rnn_out_norm"][0], 4)
    pv[:, PV_GA:PV_GA + 4] = cols(inp["attn_out_norm"][0], 4)
    pv[:, PV_GQ] = np.tile(np.asarray(inp["q_norm"][0], np.float32), 2)
    pv[:, PV_GK] = np.tile(np.asarray(inp["k_norm"][0], np.float32), 2)
    pv[:, PV_FLAG] = flag
    return pv


def _bd(inp):
    bd = np.zeros((128, 8, 128), np.float32)
    wa = np.asarray(inp["rg_w_a"][0], np.float32)
    wx = np.asarray(inp["rg_w_x"][0], np.float32)
    for c in range(4):
        for e in range(2):
            bd[64 * e:64 * e + 64, c, 64 * e:64 * e + 64] = wa[2 * c + e]
            bd[64 * e:64 * e + 64, 4 + c, 64 * e:64 * e + 64] = wx[2 * c + e]
    return bd


def make_in_maps(inp, NPRE=4, NOWN=4, ncores=8):
    x = np.asarray(inp["x"], np.float32)
    ident, cm = _consts()
    bd = _bd(inp)
    half = NOWN * TT
    shared = {
        "ident": ident, "cmat": cm, "bd": bd,
        "wg1": np.ascontiguousarray(inp["ffn1_w_gate"][0], np.float32), "wu1": np.ascontiguousarray(inp["ffn1_w_up"][0], np.float32),
        "wd1": np.ascontiguousarray(inp["ffn1_w_down"][0], np.float32), "win": np.ascontiguousarray(inp["w_in"][0], np.float32),
        "wout": np.ascontiguousarray(inp["w_out"][0], np.float32),
        "wg2": np.ascontiguousarray(inp["ffn2_w_gate"][0], np.float32), "wu2": np.ascontiguousarray(inp["ffn2_w_up"][0], np.float32),
        "wd2": np.ascontiguousarray(inp["ffn2_w_down"][0], np.float32),
    }
    pvs = [_pvec(inp, 0.0), _pvec(inp, 1.0)]
    maps = []
    for i in range(ncores):
        b, h = i // 2, i % 2
        m = dict(shared)
        m["xown"] = np.ascontiguousarray(x[b, h * half:(h + 1) * half])
        npre_rows = max(NPRE * TT, 128)
        if h == 1:
            m["xpre"] = np.ascontiguousarray(x[b, 0:npre_rows])
        else:
            m["xpre"] = np.zeros((npre_rows, D), np.float32)
        m["pvec"] = pvs[h]
        maps.append(m)
    return maps


def kernel(**inputs):
    x = np.asarray(inputs["x"])
    B, SEQ, _ = x.shape
    nc = build(4, 4)
    maps = make_in_maps(inputs, 4, 4, 8)
    res = run_bass_kernel_spmd(nc, maps, core_ids=list(range(8)))
    out = np.empty((B, SEQ, D), np.float32)
    half = SEQ // 2
    for i in range(8):
        b, h = i // 2, i % 2
        out[b, h * half:(h + 1) * half] = res.results[i]["out"]
    return out
```
